# Optimizing a Trainium2 kernel written in Bass

```python
import jax, jax.numpy as jnp
from jax import lax
import numpy as np

D_MODEL = 1024
BATCH = 8
SEQ = 4096
DEPTH = 4

N_A_LAYERS = DEPTH // 2
N_B_LAYERS = DEPTH - N_A_LAYERS
POOL_WINDOWS = (2, 4, 8, 16)
N_POOL_GROUPS = len(POOL_WINDOWS)
POOL_GROUP = D_MODEL // N_POOL_GROUPS
N_HEADS = 8
QK_NOPE = 128
QK_ROPE = 64
V_HEAD = 128
QK_HEAD = QK_NOPE + QK_ROPE
Q_RANK = 3 * D_MODEL // 8
KV_RANK = D_MODEL // 4
ROPE_THETA = 10000.0
Q_BLOCK = 128
D_FF = ((8 * D_MODEL // 3 + 127) // 128) * 128
CONV_WIDTH = 3
EPS = 1e-6
N_MOD = 6
MAX_POS_OFFSET = 1024

kernel_name = "yoco_pool_mla_adaln_convglu"


def rmsnorm(x, g):
    x32 = x.astype(jnp.float32)
    y = x32 * lax.rsqrt(jnp.mean(x32 * x32, axis=-1, keepdims=True) + EPS)
    return y.astype(x.dtype) * g


def modulate(h, shift, scale):
    return h * (1 + scale[:, None, :]) + shift[:, None, :]


def rope_tables(positions):
    inv = 1.0 / (ROPE_THETA ** (jnp.arange(0, QK_ROPE, 2, dtype=jnp.float32) / QK_ROPE))
    ang = positions.astype(jnp.float32)[..., None] * inv
    return jnp.cos(ang), jnp.sin(ang)


def apply_rope(x, cos, sin):
    x32 = x.astype(jnp.float32)
    x1, x2 = jnp.split(x32, 2, axis=-1)
    out = jnp.concatenate([x1 * cos - x2 * sin, x2 * cos + x1 * sin], axis=-1)
    return out.astype(x.dtype)


def trailing_mean_minus_self(h, w):
    s = h.shape[1]
    h32 = h.astype(jnp.float32)
    cs = jnp.cumsum(h32, axis=1)
    cs_lag = jnp.pad(cs, ((0, 0), (w, 0), (0, 0)))[:, :s]
    count = jnp.minimum(jnp.arange(1, s + 1, dtype=jnp.float32), float(w))
    mean = (cs - cs_lag) / count[None, :, None]
    return (mean - h32).astype(h.dtype)


def pool_mixer(h, w_g, b_g, scale):
    bsz, s, d = h.shape
    hg = h.reshape(bsz, s, N_POOL_GROUPS, POOL_GROUP)
    pooled = jnp.stack([trailing_mean_minus_self(hg[:, :, g], POOL_WINDOWS[g])
                        for g in range(N_POOL_GROUPS)], axis=2)
    y = jnp.einsum('bsgc,gcd->bsgd', pooled, w_g).reshape(bsz, s, d) + b_g
    return y * scale


def conv_glu_ffn(h, w_up, conv_w, conv_b, w_down):
    s = h.shape[1]
    a, v = jnp.split(h @ w_up, 2, axis=-1)
    ap = jnp.pad(a, ((0, 0), (CONV_WIDTH - 1, 0), (0, 0)))
    a = sum(ap[:, k:k + s] * conv_w[k] for k in range(CONV_WIDTH)) + conv_b
    return (jax.nn.gelu(a, approximate=False) * v) @ w_down


def shared_kv(x, kv_in_g, w_dkv, ckv_norm_g, w_uk, w_uv, cos, sin):
    bsz, s, _ = x.shape
    kv = rmsnorm(x, kv_in_g) @ w_dkv
    c_kv = rmsnorm(kv[..., :KV_RANK], ckv_norm_g)
    k_rope = apply_rope(kv[..., KV_RANK:], cos, sin)
    k_nope = (c_kv @ w_uk).reshape(bsz, s, N_HEADS, QK_NOPE)
    k = jnp.concatenate([k_nope, jnp.broadcast_to(k_rope[:, :, None, :],
                                                  (bsz, s, N_HEADS, QK_ROPE))], axis=-1)
    v = (c_kv @ w_uv).reshape(bsz, s, N_HEADS, V_HEAD)
    return k, v


def causal_block_attention(q, k, v):
    s = q.shape[1]
    q = q * (QK_HEAD ** -0.5)
    outs = []
    for i in range(s // Q_BLOCK):
        q0 = i * Q_BLOCK
        k_end = q0 + Q_BLOCK
        sc = jnp.einsum('bqhd,bkhd->bhqk', q[:, q0:k_end], k[:, :k_end],
                        preferred_element_type=jnp.float32)
        mask = jnp.arange(k_end)[None, :] <= jnp.arange(q0, k_end)[:, None]
        sc = jnp.where(mask, sc, jnp.finfo(jnp.float32).min)
        p = jax.nn.softmax(sc, axis=-1).astype(v.dtype)
        outs.append(jnp.einsum('bhqk,bkhd->bqhd', p, v[:, :k_end]))
    return jnp.concatenate(outs, axis=1)


def mla_mixer(h, k, v, w_dq, q_norm_g, w_uq, w_o, cos, sin):
    bsz, s, _ = h.shape
    c_q = rmsnorm(h @ w_dq, q_norm_g)
    q = (c_q @ w_uq).reshape(bsz, s, N_HEADS, QK_HEAD)
    q = jnp.concatenate([q[..., :QK_NOPE],
                         apply_rope(q[..., QK_NOPE:], cos[:, :, None], sin[:, :, None])], axis=-1)
    o = causal_block_attention(q, k, v)
    return o.reshape(bsz, s, N_HEADS * V_HEAD) @ w_o


def setup_inputs(seed: int = 0) -> dict:
    key = jax.random.key(seed)
    ks = jax.random.split(key, 26)
    f32 = jnp.float32
    nrm = lambda k, shape, s: jax.random.normal(k, shape, f32) * s
    d, f = D_MODEL, D_FF
    positions = (jnp.arange(SEQ, dtype=jnp.int32)[None, :]
                 + jax.random.randint(ks[2], (BATCH, 1), 0, MAX_POS_OFFSET, dtype=jnp.int32))
    return {
        "x": nrm(ks[0], (BATCH, SEQ, d), 1.0),
        "c": nrm(ks[1], (BATCH, d), 1.0),
        "positions": positions,
        "mod_w": nrm(ks[3], (DEPTH, d, N_MOD * d), d ** -0.5),
        "mod_b": nrm(ks[4], (DEPTH, N_MOD * d), 0.01),
        "norm1_g": 1.0 + nrm(ks[5], (DEPTH, d), 0.02),
        "norm2_g": 1.0 + nrm(ks[6], (DEPTH, d), 0.02),
        "pool_w": nrm(ks[7], (N_A_LAYERS, N_POOL_GROUPS, POOL_GROUP, POOL_GROUP), POOL_GROUP ** -0.5),
        "pool_b": nrm(ks[8], (N_A_LAYERS, d), 0.01),
        "pool_scale": 1.0 + nrm(ks[9], (N_A_LAYERS, d), 0.1),
        "kv_in_g": 1.0 + nrm(ks[10], (d,), 0.02),
        "w_dkv": nrm(ks[11], (d, KV_RANK + QK_ROPE), d ** -0.5),
        "ckv_norm_g": 1.0 + nrm(ks[12], (KV_RANK,), 0.02),
        "w_uk": nrm(ks[13], (KV_RANK, N_HEADS * QK_NOPE), KV_RANK ** -0.5),
        "w_uv": nrm(ks[14], (KV_RANK, N_HEADS * V_HEAD), KV_RANK ** -0.5),
        "w_dq": nrm(ks[15], (N_B_LAYERS, d, Q_RANK), d ** -0.5),
        "q_norm_g": 1.0 + nrm(ks[16], (N_B_LAYERS, Q_RANK), 0.02),
        "w_uq": nrm(ks[17], (N_B_LAYERS, Q_RANK, N_HEADS * QK_HEAD), Q_RANK ** -0.5),
        "w_o": nrm(ks[18], (N_B_LAYERS, N_HEADS * V_HEAD, d), (N_HEADS * V_HEAD) ** -0.5),
        "w_up": nrm(ks[19], (DEPTH, d, 2 * f), d ** -0.5),
        "conv_w": nrm(ks[20], (DEPTH, CONV_WIDTH, f), CONV_WIDTH ** -0.5),
        "conv_b": nrm(ks[21], (DEPTH, f), 0.01),
        "w_down": nrm(ks[22], (DEPTH, f, d), f ** -0.5),
        "final_g": 1.0 + nrm(ks[23], (d,), 0.02),
    }


def reference(x, c, positions, mod_w, mod_b, norm1_g, norm2_g, pool_w, pool_b, pool_scale,
              kv_in_g, w_dkv, ckv_norm_g, w_uk, w_uv, w_dq, q_norm_g, w_uq, w_o,
              w_up, conv_w, conv_b, w_down, final_g):
    cos, sin = rope_tables(positions)
    mods = jnp.einsum('bd,lde->lbe', jax.nn.silu(c), mod_w) + mod_b[:, None, :]
    k = v = None
    for l in range(DEPTH):
        sh1, sc1, g1, sh2, sc2, g2 = jnp.split(mods[l], N_MOD, axis=-1)
        h = modulate(rmsnorm(x, norm1_g[l]), sh1, sc1)
        if l < N_A_LAYERS:
            y = pool_mixer(h, pool_w[l], pool_b[l], pool_scale[l])
        else:
            j = l - N_A_LAYERS
            y = mla_mixer(h, k, v, w_dq[j], q_norm_g[j], w_uq[j], w_o[j], cos, sin)
        x = x + g1[:, None, :] * y
        h = modulate(rmsnorm(x, norm2_g[l]), sh2, sc2)
        x = x + g2[:, None, :] * conv_glu_ffn(h, w_up[l], conv_w[l], conv_b[l], w_down[l])
        if l == N_A_LAYERS - 1:
            k, v = shared_kv(x, kv_in_g, w_dkv, ckv_norm_g, w_uk, w_uv, cos, sin)
    return rmsnorm(x, final_g)
```

```python
import contextlib
import numpy as np
import concourse.bass as bass
import concourse.mybir as mybir
from concourse.bass_utils import run_bass_kernel_spmd

F32 = mybir.dt.float32
BF16 = mybir.dt.bfloat16
I32 = mybir.dt.int32
AF = mybir.ActivationFunctionType
ALU = mybir.AluOpType
AX = mybir.AxisListType

PE, ACT, DVE, POOL, SP = "pe", "act", "dve", "pool", "sp"
ENGS = [PE, ACT, DVE, POOL, SP]

D = 1024
S = 4096
DEPTH = 4
NCH = 8
T = 512
NTB = S // T
FF = 2816
NF = FF // 128
QR = 384
KVR = 256
NH = 8
ROPE = 64
WINDOWS = (2, 4, 8, 16)
EPS = 1e-6
NEG = -30000.0
QSCALE = 192.0 ** -0.5


class _Op:
    __slots__ = ("eng", "fn", "deps", "is_dma", "key", "sig", "needed")

    def __init__(self, eng, fn, is_dma, key):
        self.eng = eng
        self.fn = fn
        self.deps = []
        self.is_dma = is_dma
        self.key = key
        self.sig = None
        self.needed = False


class Prog:
    def __init__(self, nc):
        self.nc = nc
        self.ops = {e: [] for e in ENGS}
        self.last_w = {}
        self.readers = {}
        self.dma_cnt = {}
        self.pending_dma = []

    def _record(self, eng, fn, reads, writes, is_dma=False, key=None):
        op = _Op(eng, fn, is_dma, key)
        deps = {}
        for t in reads:
            w = self.last_w.get(t)
            if w is not None:
                deps[id(w)] = w
        for t in writes:
            w = self.last_w.get(t)
            if w is not None:
                deps[id(w)] = w
            for r in self.readers.get(t, {}).values():
                deps[id(r)] = r
        for d in deps.values():
            if d is op:
                continue
            if (not d.is_dma) and (not is_dma) and d.eng == PE and eng == PE:
                continue
            op.deps.append(d)
            d.needed = True
        for t in writes:
            self.last_w[t] = op
            self.readers[t] = {}
        rk = (eng, id(op)) if is_dma else eng
        for t in reads:
            if t not in writes:
                self.readers.setdefault(t, {})[rk] = op
        self.ops[eng].append(op)
        if is_dma:
            self.pending_dma.append(op)
        return op

    def pe(self, fn, reads=(), writes=()):
        return self._record(PE, fn, reads, writes)

    def act(self, fn, reads=(), writes=()):
        return self._record(ACT, fn, reads, writes)

    def dve(self, fn, reads=(), writes=()):
        return self._record(DVE, fn, reads, writes)

    def pool(self, fn, reads=(), writes=()):
        return self._record(POOL, fn, reads, writes)

    def on(self, eng, fn, reads=(), writes=()):
        return self._record(eng, fn, reads, writes)

    def dma(self, eng, fn, reads=(), writes=(), key=None):
        return self._record(eng, fn, reads, writes, is_dma=True, key=key)

    def barrier(self):
        lasts = []
        for e in ENGS:
            for op in reversed(self.ops[e]):
                if op.fn is not None and not op.is_dma:
                    lasts.append(op)
                    break
        dmas = list(self.pending_dma)
        self.pending_dma = []
        for e in ENGS:
            b = _Op(e, None, False, None)
            for d in lasts + dmas:
                if (not d.is_dma) and d.eng == e and e == PE:
                    continue
                b.deps.append(d)
                d.needed = True
            self.ops[e].append(b)
        self.last_w = {}
        self.readers = {}

    def emit(self):
        nc = self.nc
        for e in ENGS:
            cnt = 0
            for op in self.ops[e]:
                if op.is_dma:
                    c = self.dma_cnt.get(op.key, 0) + 16
                    self.dma_cnt[op.key] = c
                    op.sig = ("dma_" + op.key, c)
                elif op.needed and op.fn is not None:
                    cnt += 1
                    op.sig = ("eng_" + e, cnt)
        names = ["eng_" + e for e in ENGS] + ["dma_" + k for k in self.dma_cnt]
        with contextlib.ExitStack() as st:
            sems = {n: st.enter_context(nc.semaphore(n)) for n in names}
            block = st.enter_context(nc.Block())

            def make(e):
                def body(h):
                    seen = {}
                    for op in self.ops[e]:
                        for d in op.deps:
                            sn, v = d.sig
                            if seen.get(sn, 0) >= v:
                                continue
                            seen[sn] = v
                            h.wait_ge(sems[sn], v)
                        if op.fn is None:
                            continue
                        ins = op.fn(h)
                        if op.sig is not None:
                            ins.then_inc(sems[op.sig[0]], 16 if op.is_dma else 1)
                return body

            block.tensor(make(PE))
            block.scalar(make(ACT))
            block.vector(make(DVE))
            block.gpsimd(make(POOL))
            block.sync(make(SP))


class Arena:
    def __init__(self, nc, nbytes):
        self.t = nc.alloc_sbuf_tensor("arena", [128, nbytes // 2], BF16)
        self.nbytes = nbytes
        self.off = 0

    def alloc(self, shape, dtype, parts=128):
        esz = 4 if dtype in (F32, I32) else 2
        n = 1
        for s in shape:
            n *= s
        nb = (n * esz + 63) // 64 * 64
        assert self.off + nb <= self.nbytes, f"arena overflow {self.off}+{nb}>{self.nbytes}"
        v = self.t[0:parts, self.off // 2:(self.off + n * esz) // 2]
        if esz == 4:
            v = v.bitcast(dtype)
        self.off += nb
        if len(shape) == 2:
            v = v.rearrange("p (a b) -> p a b", a=shape[0])
        elif len(shape) == 3:
            v = v.rearrange("p (a b c) -> p a b c", a=shape[0], b=shape[1])
        return v

    def mark(self):
        return self.off

    def reset(self, m):
        self.off = m


def _vec_layout():
    cols = {}
    off = 0
    for name, n in [("cT", 8), ("mod_b", 4 * 48), ("n1g", 32), ("n2g", 32), ("pool_b", 16),
                    ("pool_s", 16), ("kvg", 8), ("ckvg", 2), ("qng", 6), ("convw", 4 * 3 * NF),
                    ("convb", 4 * NF), ("fg", 8), ("eps", 1), ("invf", 1), ("sgn", 1),
                    ("invc", 64), ("one", 1)]:
        cols[name] = off
        off += n
    return cols, off


VCOL, NV = _vec_layout()


def _chunks(v):
    v = np.asarray(v, np.float32)
    lead = v.shape[:-1]
    n = v.shape[-1] // 128
    v = v.reshape(lead + (n, 128))
    v = np.moveaxis(v, -1, 0)
    return np.ascontiguousarray(v.reshape(128, -1))


def _build_vecs(inp, b):
    V = np.zeros((128, NV), np.float32)

    def put(name, arr):
        V[:, VCOL[name]:VCOL[name] + arr.shape[1]] = arr

    put("cT", _chunks(inp["c"][b]))
    put("mod_b", _chunks(inp["mod_b"]))
    put("n1g", _chunks(inp["norm1_g"]))
    put("n2g", _chunks(inp["norm2_g"]))
    put("pool_b", _chunks(inp["pool_b"]))
    put("pool_s", _chunks(inp["pool_scale"]))
    put("kvg", _chunks(inp["kv_in_g"]))
    put("ckvg", _chunks(inp["ckv_norm_g"]))
    put("qng", _chunks(inp["q_norm_g"]))
    put("convw", _chunks(inp["conv_w"]))
    put("convb", _chunks(inp["conv_b"]))
    put("fg", _chunks(inp["final_g"]))
    V[:, VCOL["eps"]] = EPS
    V[:, VCOL["one"]] = 1.0
    inv = (1.0 / (10000.0 ** (np.arange(0, ROPE, 2, dtype=np.float32) / ROPE))).astype(np.float32)
    V[0:32, VCOL["invf"]] = inv
    V[32:64, VCOL["invf"]] = inv
    V[0:32, VCOL["sgn"]] = -1.0
    V[32:64, VCOL["sgn"]] = 1.0
    for g, w in enumerate(WINDOWS):
        for t in range(16):
            V[:, VCOL["invc"] + g * 16 + t] = 1.0 / min(t + 1, w)
    return V


def _consts():
    ident = np.eye(128, dtype=np.float32)
    q = np.arange(128)[:, None]
    k = np.arange(128)[None, :]
    maskq = np.where(k <= q, 0.0, NEG).astype(np.float32)
    maskt = np.ascontiguousarray(maskq.T)
    return np.concatenate([ident, maskq, maskt], axis=1)


def _maskg():
    mg = np.zeros((128, 4, 4, 128), np.float32)
    k = np.arange(128)[:, None]
    q = np.arange(128)[None, :]
    tri = np.where(k <= q, 0.0, NEG).astype(np.float32)
    for r in range(4):
        for ip in range(4):
            if ip < r:
                mg[:, r, ip, :] = NEG
            elif ip == r:
                mg[:, r, ip, :] = tri
    return np.ascontiguousarray(mg.reshape(128, 4 * 512))


def build_program(stop_after=None):
    nc = bass.Bass("TRN2", target_bir_lowering=False)
    dr = {}

    def din(name, shape, dt=F32):
        dr[name] = nc.dram_tensor(name, list(shape), dt, kind="ExternalInput").ap()
        return dr[name]

    x_d = din("x", [S, D])
    pos_d = din("pos", [1, S], I32)
    vecs_d = din("vecs", [128, NV])
    cst_d = din("cst", [128, 384])
    din("maskg", [128, 4 * T])
    mod_w = din("mod_w", [DEPTH, D, 6 * D])
    pool_w = din("pool_w", [2, 4, 256, 256])
    w_dkv = din("w_dkv", [D, KVR + ROPE])
    w_uk = din("w_uk", [KVR, NH * 128])
    w_uv = din("w_uv", [KVR, NH * 128])
    w_dq = din("w_dq", [2, D, QR])
    w_uq = din("w_uq", [2, QR, NH * 192])
    w_o = din("w_o", [2, NH * 128, D])
    w_up = din("w_up", [DEPTH, D, 2 * FF])
    w_down = din("w_down", [DEPTH, FF, D])
    y_d = nc.dram_tensor("y", [S, D], F32, kind="ExternalOutput").ap()
    cs_d = nc.dram_tensor("cs_scr", [2, 64, S], F32).ap()
    ckv_d = nc.dram_tensor("ckv_scr", [128, 2, S], BF16).ap()
    kr_d = nc.dram_tensor("kr_scr", [64, S], BF16).ap()
    cq_d = nc.dram_tensor("cq_scr", [128, 3, S], BF16).ap()
    oT_d = nc.dram_tensor("oT_scr", [128, NH, S], BF16).ap()

    A = Arena(nc, 211200)
    P = Prog(nc)
    ps = [nc.alloc_psum_tensor(f"ps{i}", [128, 512], F32) for i in range(8)]

    xT = A.alloc([NCH, S], F32)
    vecs = A.alloc([NV], F32)
    cst = A.alloc([384], F32)
    mods = A.alloc([DEPTH * 48], F32)
    coef = A.alloc([DEPTH * 2 * 8], F32)
    pcoef = A.alloc([2 * 2 * 8], F32)
    identb = A.alloc([128], BF16)
    onesb = A.alloc([128], BF16)
    maskqb = A.alloc([128], BF16)
    masktb = A.alloc([128], BF16)
    siluc = A.alloc([8], BF16)
    identf = cst[:, 0:128]
    PH = A.mark()

    def vc(name, i=0, n=1, parts=128):
        o = VCOL[name] + i
        return vecs[0:parts, o:o + n]

    def mcol(l, k, c):
        o = l * 48 + k * 8 + c
        return mods[:, o:o + 1]

    def acol(l, which, c):
        o = (l * 2 + which) * 8 + c
        return coef[:, o:o + 1]

    def xtok(c, t0, t1):
        return [f"x{c}_{b}" for b in range(t0 // T, (t1 + T - 1) // T)]

    eps_col = vc("eps")

    P.dma(SP, lambda e: e.dma_start(out=vecs, in_=vecs_d), writes=["vecs"], key="c0")
    P.dma(SP, lambda e: e.dma_start(out=cst, in_=cst_d), writes=["cst"], key="c1")
    P.dve(lambda e: e.tensor_copy(identb, cst[:, 0:128]), reads=["cst"], writes=["identb"])
    P.dve(lambda e: e.tensor_copy(maskqb, cst[:, 128:256]), reads=["cst"], writes=["maskqb"])
    P.dve(lambda e: e.tensor_copy(masktb, cst[:, 256:384]), reads=["cst"], writes=["masktb"])
    P.dve(lambda e: e.memset(onesb, 1.0), writes=["onesb"])
    P.act(lambda e: e.activation(siluc, vc("cT", 0, 8), AF.Silu), reads=["vecs"], writes=["siluc"])

    xin = [A.alloc([4, D], F32) for _ in range(2)]
    for tg in range(NTB):
        xb = xin[tg % 2]
        P.dma(SP, lambda e, xb=xb, tg=tg: e.dma_start(
            out=xb, in_=x_d[tg * T:(tg + 1) * T, :].rearrange("(i p) d -> p i d", p=128)),
            writes=[f"xin{tg % 2}"], key=f"xin{tg % 2}")
        for c in range(NCH):
            bank = ps[c % 4]
            for i in range(4):
                P.pe(lambda e, bank=bank, xb=xb, i=i, c=c: e.transpose(
                    bank[:, i * 128:(i + 1) * 128], xb[:, i, c * 128:(c + 1) * 128], identf),
                    reads=[f"xin{tg % 2}", "cst"], writes=[f"ps{c % 4}"])
            dst = xT[:, c, tg * T:(tg + 1) * T]
            if c % 2 == 0:
                P.dve(lambda e, dst=dst, bank=bank: e.tensor_copy(dst, bank[:]),
                      reads=[f"ps{c % 4}"], writes=xtok(c, tg * T, (tg + 1) * T))
            else:
                P.act(lambda e, dst=dst, bank=bank: e.activation(dst, bank[:], AF.Copy),
                      reads=[f"ps{c % 4}"], writes=xtok(c, tg * T, (tg + 1) * T))

    def mods_pieces(l, bank, btok, mw=None, wcols=512):
        if mw is None:
            mw = [A.alloc([8, wcols], BF16) for _ in range(2)]
        npc = wcols // 128
        out = []
        for q in range(6 * D // wcols):
            def piece(q=q):
                buf = mw[q % 2]
                mtok = f"mw{q % 2}"
                P.dma(POOL, lambda e: e.dma_start(
                    out=buf, in_=mod_w[l][:, q * wcols:(q + 1) * wcols].rearrange("(kc p) n -> p kc n", p=128)),
                    writes=[mtok], key=mtok)
                for jj in range(npc):
                    j = q * npc + jj
                    for kc in range(8):
                        P.pe(lambda e, jj=jj, kc=kc, j=j: e.matmul(
                            bank[:, j:j + 1], buf[:, kc, jj * 128:(jj + 1) * 128], siluc[:, kc:kc + 1],
                            start=(kc == 0), stop=(kc == 7)),
                            reads=[mtok, "siluc"], writes=[btok])
            out.append(piece)

        def fin():
            P.dve(lambda e: e.tensor_tensor(mods[:, l * 48:(l + 1) * 48], bank[:, 0:48], vc("mod_b", l * 48, 48),
                                            ALU.add), reads=[btok, "vecs"], writes=["mods"])
            for which, (gname, k) in enumerate([("n1g", 1), ("n2g", 4)]):
                dst = coef[:, (l * 2 + which) * 8:(l * 2 + which) * 8 + 8]
                src = mods[:, l * 48 + k * 8:l * 48 + k * 8 + 8]
                P.dve(lambda e, dst=dst, src=src, gname=gname: e.scalar_tensor_tensor(
                    dst, src, 1.0, vc(gname, l * 8, 8), ALU.add, ALU.mult),
                    reads=["mods", "vecs"], writes=["coef"])
            if l < 2:
                pc = pcoef[:, (l * 2) * 8:(l * 2) * 8 + 8]
                pb = pcoef[:, (l * 2 + 1) * 8:(l * 2 + 1) * 8 + 8]
                g1 = mods[:, l * 48 + 16:l * 48 + 24]
                P.dve(lambda e: e.tensor_tensor(pc, g1, vc("pool_s", l * 8, 8), ALU.mult),
                      reads=["mods", "vecs"], writes=["pcoef"])
                P.dve(lambda e: e.tensor_tensor(pb, pc, vc("pool_b", l * 8, 8), ALU.mult),
                      reads=["pcoef", "vecs"], writes=["pcoef"])
        out.append(fin)
        return out

    def mods_ops(l, bank, btok, mw=None, wcols=512):
        for f_ in mods_pieces(l, bank, btok, mw, wcols):
            f_()

    mw_init = [A.alloc([8, 512], BF16) for _ in range(2)]
    mods_ops(0, ps[5], "ps5", mw_init)
    mods_ops(1, ps[4], "ps4", mw_init)

    HS = S // 8
    posf = A.alloc([HS], F32, parts=64)
    ang = A.alloc([HS], F32, parts=64)
    kf = A.alloc([HS], F32, parts=64)
    ki = A.alloc([HS], I32, parts=64)
    rr = A.alloc([HS], F32, parts=64)
    C1 = 6.28125
    C2 = float(2 * np.pi - 6.28125)
    for hf in range(S // HS):
        P.dma(SP, lambda e, hf=hf: e.dma_start(
            out=ki, in_=pos_d[:, hf * HS:(hf + 1) * HS].partition_broadcast(64)),
            writes=["ki"], key="pos")
        P.dve(lambda e: e.tensor_copy(posf, ki), reads=["ki"], writes=["posf"])
        P.dve(lambda e: e.tensor_scalar(ang, posf, vc("invf", 0, 1, 64), None, ALU.mult),
              reads=["posf", "vecs"], writes=["ang"])
        for which, shift in enumerate([float(np.pi / 2), 0.0]):
            P.dve(lambda e, shift=shift: e.tensor_scalar(rr, ang, shift, None, ALU.add),
                  reads=["ang"], writes=["rr"])
            P.dve(lambda e: e.tensor_scalar(kf, rr, float(1 / (2 * np.pi)), None, ALU.mult),
                  reads=["rr"], writes=["kf"])
            P.dve(lambda e: e.tensor_copy(ki, kf), reads=["kf"], writes=["ki"])
            P.dve(lambda e: e.tensor_copy(kf, ki), reads=["ki"], writes=["kf"])
            P.dve(lambda e: e.scalar_tensor_tensor(rr, kf, -C1, rr, ALU.mult, ALU.add),
                  reads=["kf", "rr"], writes=["rr"])
            P.dve(lambda e: e.scalar_tensor_tensor(rr, kf, -C2, rr, ALU.mult, ALU.add),
                  reads=["kf", "rr"], writes=["rr"])
            P.dve(lambda e: e.tensor_scalar(kf, rr, float(np.pi), float(-2 * np.pi), ALU.is_gt, ALU.mult),
                  reads=["rr"], writes=["kf"])
            P.dve(lambda e: e.tensor_tensor(rr, rr, kf, ALU.add), reads=["kf", "rr"], writes=["rr"])
            P.dve(lambda e: e.tensor_scalar(kf, rr, float(-np.pi), float(2 * np.pi), ALU.is_lt, ALU.mult),
                  reads=["rr"], writes=["kf"])
            P.dve(lambda e: e.tensor_tensor(rr, rr, kf, ALU.add), reads=["kf", "rr"], writes=["rr"])
            if which == 0:
                P.act(lambda e: e.activation(rr, rr, AF.Sin), reads=["rr"], writes=["rr"])
            else:
                P.act(lambda e: e.activation(rr, rr, AF.Sin, scale=vc("sgn", 0, 1, 64)), reads=["rr", "vecs"],
                      writes=["rr"])
            P.dma(SP, lambda e, which=which, hf=hf: e.dma_start(
                out=cs_d[which, :, hf * HS:(hf + 1) * HS], in_=rr), reads=["rr"], writes=["cs_d"], key="cso")
    P.barrier()
    A.reset(PH)

    def rstd_block(srcs, dim, sq, ss_bank, ss_tok, rs, rstd, rtok, src_reads, tg=""):
        nsrc = len(srcs)
        nb = len(sq)
        for i, src in enumerate(srcs):
            P.act(lambda e, src=src, i=i: e.activation(sq[i % nb], src, AF.Square),
                  reads=src_reads[i], writes=[f"sq{i % nb}{tg}"])
            P.pe(lambda e, i=i: e.matmul(ss_bank[:], onesb, sq[i % nb], start=(i == 0), stop=(i == nsrc - 1)),
                 reads=[f"sq{i % nb}{tg}", "onesb"], writes=[ss_tok])
        P.act(lambda e: e.activation(rs, ss_bank[:], AF.Ln, bias=eps_col, scale=1.0 / dim),
              reads=[ss_tok, "vecs"], writes=["rs" + tg])
        P.act(lambda e: e.activation(rstd, rs, AF.Exp, scale=-0.5), reads=["rs" + tg], writes=[rtok])

    def dump_x(normed):
        m = A.mark()
        sq = [A.alloc([T], BF16) for _ in range(8)]
        rs = A.alloc([T], F32)
        rstd = A.alloc([T], F32)
        o32 = A.alloc([NCH, T], F32)
        stage = A.alloc([4, D], F32)
        for tb in range(NTB):
            t0, t1 = tb * T, (tb + 1) * T
            if normed:
                rstd_block([xT[:, c, t0:t1] for c in range(NCH)], D, sq, ps[7], "ps7", rs, rstd, "rstd",
                           [xtok(c, t0, t1) for c in range(NCH)])
                for c in range(NCH):
                    P.dve(lambda e, c=c, t0=t0, t1=t1: e.scalar_tensor_tensor(
                        o32[:, c, :], xT[:, c, t0:t1], vc("fg", c), rstd, ALU.mult, ALU.mult),
                        reads=xtok(c, t0, t1) + ["rstd", "vecs"], writes=[f"o32_{c}"])
            for i in range(4):
                for hh in range(2):
                    bank = ps[(i * 2 + hh) % 4]
                    btok = f"ps{(i * 2 + hh) % 4}"
                    for cc in range(4):
                        c = hh * 4 + cc
                        src = o32[:, c, i * 128:(i + 1) * 128] if normed else xT[:, c, t0 + i * 128:t0 + (i + 1) * 128]
                        rd = [f"o32_{c}"] if normed else xtok(c, t0, t1)
                        P.pe(lambda e, bank=bank, src=src, cc=cc: e.transpose(
                            bank[:, cc * 128:(cc + 1) * 128], src, identf), reads=rd + ["cst"], writes=[btok])
                    dst = stage[:, i, hh * 512:(hh + 1) * 512]
                    if hh == 0:
                        P.dve(lambda e, dst=dst, bank=bank: e.tensor_copy(dst, bank[:]), reads=[btok],
                              writes=[f"stage{i}"])
                    else:
                        P.act(lambda e, dst=dst, bank=bank: e.activation(dst, bank[:], AF.Copy), reads=[btok],
                              writes=[f"stage{i}"])
            P.dma(SP, lambda e, t0=t0, t1=t1: e.dma_start(
                out=y_d[t0:t1, :].rearrange("(i p) d -> p i d", p=128), in_=stage),
                reads=[f"stage{i}" for i in range(4)], writes=["y"], key="yout")
        P.barrier()
        A.reset(m)

    def finish():
        P.emit()
        return nc

    def ffn_phase(l):
        m = A.mark()
        NB = 3
        sq = [A.alloc([T], BF16) for _ in range(2)]
        hT = A.alloc([NCH, T], BF16)
        u = A.alloc([NF, T], BF16)
        NW = 4
        wup = [A.alloc([8, 2, 128], BF16) for _ in range(NW)]
        wd = [A.alloc([NF, 128], BF16) for _ in range(2)]
        a_sb = [A.alloc([T + 2], F32) for _ in range(NB)]
        t1b = [A.alloc([T], F32) for _ in range(NB)]
        halo = A.alloc([NF, 2], F32)
        tmp, tmptok = t1b[0], "t10"
        rs, rstok = t1b[1], "t11"
        rstd = rs
        P.dve(lambda e: e.memset(halo, 0.0), writes=[f"halo{j}" for j in range(NF)])
        wup_src = w_up[l].rearrange("(kc p) (h j m) -> p kc h j m", p=128, h=2, m=128)
        wd_src = w_down[l].rearrange("(j p) (c m) -> p j c m", p=128, m=128)

        up_list = [(tb, j) for tb in range(NTB) for j in range(NF)]
        dn_list = [(tb, c) for tb in range(NTB) for c in range(NCH)]
        st = {"up": 0, "dn": 0}

        def issue_up(upto):
            while st["up"] < min(upto, len(up_list)):
                n = st["up"]
                _, j = up_list[n]
                wb = wup[n % NW]
                wtok = f"wup{n % NW}"
                P.dma(POOL, lambda e, wb=wb, j=j: e.dma_start(out=wb[:, :, 0, :], in_=wup_src[:, :, 0, j, :]),
                      writes=[wtok + "a"], key=wtok + "a")
                P.dma(POOL, lambda e, wb=wb, j=j: e.dma_start(out=wb[:, :, 1, :], in_=wup_src[:, :, 1, j, :]),
                      writes=[wtok + "v"], key=wtok + "v")
                st["up"] += 1

        def issue_dn(upto):
            while st["dn"] < min(upto, len(dn_list)):
                n = st["dn"]
                _, c = dn_list[n]
                wdb = wd[n % 2]
                P.dma(POOL, lambda e, wdb=wdb, c=c: e.dma_start(out=wdb, in_=wd_src[:, :, c, :]),
                      writes=[f"wd{n % 2}"], key=f"wd{n % 2}")
                st["dn"] += 1

        def stats_h(tb):
            t0, t1 = tb * T, (tb + 1) * T
            srcs = [xT[:, c, t0:t1] for c in range(NCH)]
            for i, src in enumerate(srcs):
                P.act(lambda e, src=src, i=i: e.activation(sq[i % 2], src, AF.Square),
                      reads=xtok(i, t0, t1), writes=[f"sq{i % 2}"])
                P.pe(lambda e, i=i: e.matmul(ps[0][:], onesb, sq[i % 2], start=(i == 0), stop=(i == NCH - 1)),
                     reads=[f"sq{i % 2}", "onesb"], writes=["ps0"])
            P.act(lambda e: e.activation(rs, ps[0][:], AF.Ln, bias=eps_col, scale=1.0 / D),
                  reads=["ps0", "vecs"], writes=[rstok])
            P.act(lambda e: e.activation(rstd, rs, AF.Exp, scale=-0.5), reads=[rstok], writes=[rstok])
            for c in range(NCH):
                tm, tmt = (t1b[0], "t10") if c % 2 == 0 else (t1b[2], "t12")
                P.dve(lambda e, c=c, t0=t0, t1=t1, tm=tm: e.tensor_tensor(tm, xT[:, c, t0:t1], rstd, ALU.mult),
                      reads=xtok(c, t0, t1) + [rstok], writes=[tmt])
                P.act(lambda e, c=c, tm=tm: e.activation(hT[:, c, :], tm, AF.Identity,
                                                         bias=mcol(l, 3, c), scale=acol(l, 1, c)),
                      reads=[tmt, "mods", "coef"], writes=[f"hT{c}"])

        def down(tb, c):
            t0, t1 = tb * T, (tb + 1) * T
            n = tb * NCH + c
            issue_dn(n + 2)
            wdb = wd[n % 2]
            wdtok = f"wd{n % 2}"
            yb = ps[6 + c % 2]
            ytok = f"ps{6 + c % 2}"
            for j in range(NF):
                P.pe(lambda e, yb=yb, wdb=wdb, j=j: e.matmul(yb[:], wdb[:, j, :], u[:, j, :],
                                                              start=(j == 0), stop=(j == NF - 1)),
                     reads=[wdtok, f"u{j}"], writes=[ytok])
            P.dve(lambda e, yb=yb, c=c, t0=t0, t1=t1: e.scalar_tensor_tensor(
                xT[:, c, t0:t1], yb[:], mcol(l, 5, c), xT[:, c, t0:t1], ALU.mult, ALU.add),
                reads=[ytok, "mods"] + xtok(c, t0, t1), writes=xtok(c, t0, t1))

        issue_up(NW - 1)
        stats_h(0)
        pend2 = [None]

        def stage2_flush():
            if pend2[0] is not None:
                pend2[0]()
                pend2[0] = None

        for tb in range(NTB):
            for j in range(NF):
                n = tb * NF + j
                issue_up(n + NW)
                if j == NF - 6:
                    issue_dn(tb * NCH + 2)
                wb = wup[n % NW]
                wtok = f"wup{n % NW}"
                ab, vb = ps[n % NB], ps[3 + n % NB]
                atok, vtok = f"ps{n % NB}", f"ps{3 + n % NB}"
                for kc in range(8):
                    P.pe(lambda e, ab=ab, wb=wb, kc=kc: e.matmul(ab[:], wb[:, kc, 0, :], hT[:, kc, :],
                                                                  start=(kc == 0), stop=(kc == 7)),
                         reads=[wtok + "a", f"hT{kc}"], writes=[atok])
                for kc in range(8):
                    P.pe(lambda e, vb=vb, wb=wb, kc=kc: e.matmul(vb[:], wb[:, kc, 1, :], hT[:, kc, :],
                                                                  start=(kc == 0), stop=(kc == 7)),
                         reads=[wtok + "v", f"hT{kc}"], writes=[vtok])
                asb = a_sb[n % NB]
                t1_ = t1b[n % NB]
                astok, ttok = f"asb{n % NB}", f"t1{n % NB}"
                cw = lambda k, j=j: vc("convw", (l * 3 + k) * NF + j)
                cb = vc("convb", l * NF + j)
                P.dve(lambda e, asb=asb, j=j: e.tensor_copy(asb[:, 0:2], halo[:, j, :]),
                      reads=[f"halo{j}"], writes=[astok + "h"])
                P.act(lambda e, asb=asb, ab=ab: e.activation(asb[:, 2:T + 2], ab[:], AF.Copy),
                      reads=[atok], writes=[astok])
                P.act(lambda e, t1_=t1_, ab=ab, cw=cw, cb=cb: e.activation(t1_, ab[:], AF.Identity,
                                                                          bias=cb, scale=cw(2)),
                      reads=[atok, "vecs"], writes=[ttok])
                stage2_flush()
                P.dve(lambda e, t1_=t1_, asb=asb, cw=cw: e.scalar_tensor_tensor(
                    t1_, asb[:, 1:T + 1], cw(1), t1_, ALU.mult, ALU.add),
                    reads=[astok, astok + "h", ttok, "vecs"], writes=[ttok])
                P.dve(lambda e, t1_=t1_, asb=asb, cw=cw: e.scalar_tensor_tensor(
                    t1_, asb[:, 0:T], cw(0), t1_, ALU.mult, ALU.add),
                    reads=[astok, astok + "h", ttok, "vecs"], writes=[ttok])
                P.dve(lambda e, asb=asb, j=j: e.tensor_copy(halo[:, j, :], asb[:, T:T + 2]),
                      reads=[astok], writes=[f"halo{j}"])

                def stage2(t1_=t1_, vb=vb, j=j, ttok=ttok, vtok=vtok):
                    P.act(lambda e: e.activation(t1_, t1_, AF.Gelu), reads=[ttok], writes=[ttok])
                    P.dve(lambda e: e.tensor_tensor(u[:, j, :], t1_, vb[:], ALU.mult),
                          reads=[ttok, vtok], writes=[f"u{j}"])
                pend2[0] = stage2
            stage2_flush()
            down(tb, 0)
            if tb + 1 < NTB:
                stats_h(tb + 1)
            for c in range(1, NCH):
                down(tb, c)
        P.barrier()
        A.reset(m)

    def pool_phase(l):
        m = A.mark()
        sq = [A.alloc([T], BF16) for _ in range(4)]
        rs = A.alloc([T], F32)
        rstd_all = A.alloc([S], F32)
        HW = S // 2
        hp = A.alloc([16 + HW], F32)
        sA = A.alloc([16 + HW], F32)
        sB = A.alloc([16 + HW], F32)
        pooled = A.alloc([2, S], BF16)
        pw = A.alloc([4, 2, 256], BF16)
        ytmp = [A.alloc([T], F32) for _ in range(2)]
        fix = A.alloc([16], F32)
        pwf = sB[:, 0:2048].rearrange("p (g k n) -> p g k n", g=4, k=2)
        P.dma(SP, lambda e: e.dma_start(out=pwf, in_=pool_w[l].rearrange("g (kc p) n -> p g kc n", p=128)),
              writes=["sB"], key="pw")
        for g_ in range(4):
            for kc_ in range(2):
                P.dve(lambda e, g_=g_, kc_=kc_: e.tensor_scalar(
                    pw[:, g_, kc_, :], pwf[:, g_, kc_, :], acol(l, 0, 2 * g_ + kc_), None, ALU.mult),
                    reads=["sB", "coef"], writes=["pw"])
        for tb in range(NTB):
            t0, t1 = tb * T, (tb + 1) * T
            rstd_block([xT[:, c, t0:t1] for c in range(NCH)], D, sq, ps[6], "ps6", rs, rstd_all[:, t0:t1],
                       f"rstd{tb}", [xtok(c, t0, t1) for c in range(NCH)])
        P.dve(lambda e: e.memset(hp[:, 0:16], 0.0), writes=["hp"])
        P.dve(lambda e: e.memset(sA[:, 0:16], 0.0), writes=["sA"])
        P.dve(lambda e: e.memset(sB[:, 0:16], 0.0), writes=["sB"])
        rall = [f"rstd{tb}" for tb in range(NTB)]
        ny = 0
        for g in range(4):
            w = WINDOWS[g]
            for mo in range(2):
                c = 2 * g + mo
                for hf in range(2):
                    base = hf * HW
                    lo = 16 if hf == 0 else 0
                    tlo = base - 16 + lo
                    ncol = 16 + HW - lo
                    xt = xtok(c, max(tlo, 0), base + HW)
                    if hf == 0:
                        P.dve(lambda e: e.memset(hp[:, 0:16], 0.0), reads=["hp"], writes=["hp"])
                    P.dve(lambda e, lo=lo, tlo=tlo, ncol=ncol, c=c: e.tensor_tensor(
                        hp[:, lo:lo + ncol], xT[:, c, tlo:tlo + ncol], rstd_all[:, tlo:tlo + ncol], ALU.mult),
                        reads=xt + rall, writes=["hp"])
                    src, stok = hp, "hp"
                    bufs = [(sA, "sA"), (sB, "sB")]
                    sh = 1
                    k = 0
                    W_ = 16 + HW
                    while sh < w:
                        dstb, dtok = bufs[k % 2]
                        first = 2 * sh - 1
                        P.dve(lambda e, dstb=dstb, src=src, sh=sh, first=first: e.tensor_tensor(
                            dstb[:, first:W_], src[:, first:W_], src[:, first - sh:W_ - sh], ALU.add),
                            reads=[stok], writes=[dtok])
                        src, stok = dstb, dtok
                        sh *= 2
                        k += 1
                    P.dve(lambda e, src=src, mo=mo, base=base, w=w: e.scalar_tensor_tensor(
                        pooled[:, mo, base:base + HW], src[:, 16:16 + HW], 1.0 / w, hp[:, 16:16 + HW],
                        ALU.mult, ALU.subtract),
                        reads=[stok, "hp"], writes=[f"pooled{mo}_{hf}"])
                    if hf == 0:
                        P.dve(lambda e, src=src, g=g: e.tensor_tensor(
                            fix, src[:, 16:32], vc("invc", g * 16, 16), ALU.mult),
                            reads=[stok, "vecs"], writes=["fix"])
                        P.dve(lambda e, mo=mo: e.tensor_tensor(
                            pooled[:, mo, 0:16], fix, hp[:, 16:32], ALU.subtract),
                            reads=["fix", "hp", f"pooled{mo}_0"], writes=[f"pooled{mo}_0"])
            for tb in range(NTB):
                t0, t1 = tb * T, (tb + 1) * T
                hfb = tb // (NTB // 2)
                for mo in range(2):
                    c = 2 * g + mo
                    yb = ps[ny % 2]
                    ytok = f"ps{ny % 2}"
                    yt = ytmp[ny % 2]
                    yttok = f"ytmp{ny % 2}"
                    ny += 1
                    for kc in range(2):
                        P.pe(lambda e, yb=yb, kc=kc, mo=mo, g=g, t0=t0, t1=t1: e.matmul(
                            yb[:], pw[:, g, kc, mo * 128:(mo + 1) * 128], pooled[:, kc, t0:t1],
                            start=(kc == 0), stop=(kc == 1)),
                            reads=["pw", f"pooled{kc}_{hfb}"], writes=[ytok])
                    P.act(lambda e, yb=yb, yt=yt, c=c: e.activation(
                        yt, yb[:], AF.Identity, bias=pcoef[:, (l * 2 + 1) * 8 + c:(l * 2 + 1) * 8 + c + 1],
                        scale=pcoef[:, (l * 2) * 8 + c:(l * 2) * 8 + c + 1]),
                        reads=[ytok, "pcoef"], writes=[yttok])
                    P.dve(lambda e, yt=yt, c=c, t0=t0, t1=t1: e.tensor_tensor(
                        xT[:, c, t0:t1], xT[:, c, t0:t1], yt, ALU.add),
                        reads=[yttok] + xtok(c, t0, t1), writes=xtok(c, t0, t1))
        P.barrier()
        A.reset(m)

    def kv_phase():
        m = A.mark()
        sq = [A.alloc([T], BF16) for _ in range(8)]
        rs = A.alloc([T], F32)
        rstd = A.alloc([T], F32)
        hT = A.alloc([NCH, T], BF16)
        wkv = A.alloc([8, KVR + 2 * ROPE], BF16)
        ckvb = A.alloc([2, T], BF16)
        csb = A.alloc([2, T], F32, parts=64)
        t1_ = A.alloc([T], F32, parts=64)
        t2_ = A.alloc([T], F32, parts=64)
        krb = A.alloc([T], BF16, parts=64)
        src = w_dkv.rearrange("(kc p) n -> p kc n", p=128)
        P.dma(POOL, lambda e: e.dma_start(out=wkv[:, :, 0:320], in_=src), writes=["wkv"], key="wkv")
        P.dma(POOL, lambda e: e.dma_start(out=wkv[:, :, 320:352], in_=src[:, :, 288:320]), writes=["wkv"], key="wkv")
        P.dma(POOL, lambda e: e.dma_start(out=wkv[:, :, 352:384], in_=src[:, :, 256:288]), writes=["wkv"], key="wkv")
        mpieces = mods_pieces(2, ps[5], "ps5")
        for tb in range(NTB):
            t0, t1 = tb * T, (tb + 1) * T
            rstd_block([xT[:, c, t0:t1] for c in range(NCH)], D, sq, ps[6], "ps6", rs, rstd, "rstd",
                       [xtok(c, t0, t1) for c in range(NCH)])
            for c in range(NCH):
                P.dve(lambda e, c=c, t0=t0, t1=t1: e.scalar_tensor_tensor(
                    hT[:, c, :], xT[:, c, t0:t1], vc("kvg", c), rstd, ALU.mult, ALU.mult),
                    reads=xtok(c, t0, t1) + ["rstd", "vecs"], writes=[f"hT{c}"])
            P.dma(SP, lambda e, t0=t0, t1=t1: e.dma_start(out=csb, in_=cs_d[:, :, t0:t1].rearrange("w p t -> p w t")),
                  writes=["csb"], key="csb")
            outs = [(ps[0], "ps0", 0, 128, 128), (ps[1], "ps1", 128, 256, 128),
                    (ps[2], "ps2", 256, 320, 64), (ps[3], "ps3", 320, 384, 64)]
            for bank, btok, c0, c1, mm in outs:
                for kc in range(8):
                    P.pe(lambda e, bank=bank, c0=c0, c1=c1, mm=mm, kc=kc: e.matmul(
                        bank[0:mm, :], wkv[:, kc, c0:c1], hT[:, kc, :], start=(kc == 0), stop=(kc == 7)),
                        reads=["wkv", f"hT{kc}"], writes=[btok])
            rstd_block([ps[0][:], ps[1][:]], KVR, sq, ps[7], "ps7", rs, rstd, "rstd", [["ps0"], ["ps1"]])
            for mi in range(2):
                P.dve(lambda e, mi=mi: e.scalar_tensor_tensor(
                    ckvb[:, mi, :], ps[mi][:], vc("ckvg", mi), rstd, ALU.mult, ALU.mult),
                    reads=[f"ps{mi}", "rstd", "vecs"], writes=["ckvb"])
            P.dma(SP, lambda e, t0=t0, t1=t1: e.dma_start(out=ckv_d[:, :, t0:t1], in_=ckvb),
                  reads=["ckvb"], writes=["ckv_d"], key="ckvo")
            P.dve(lambda e: e.tensor_tensor(t1_, ps[2][0:64, :], csb[:, 0, :], ALU.mult),
                  reads=["ps2", "csb"], writes=["t1"])
            P.dve(lambda e: e.tensor_tensor(t2_, ps[3][0:64, :], csb[:, 1, :], ALU.mult),
                  reads=["ps3", "csb"], writes=["t2"])
            P.dve(lambda e: e.tensor_tensor(krb, t1_, t2_, ALU.add), reads=["t1", "t2"], writes=["krb"])
            P.dma(SP, lambda e, t0=t0, t1=t1: e.dma_start(out=kr_d[:, t0:t1], in_=krb),
                  reads=["krb"], writes=["kr_d"], key="kro")
            for _ in range(2):
                if mpieces:
                    mpieces.pop(0)()
        while mpieces:
            mpieces.pop(0)()
        P.barrier()
        A.reset(m)

    def mla_q_phase(l):
        jl = l - 2
        m = A.mark()
        B2 = []
        sq8 = [A.alloc([T], BF16) for _ in range(8)]
        tmp8 = [A.alloc([T], F32) for _ in range(8)]
        for par in range(2):
            B2.append(dict(sq=sq8, rs=A.alloc([T], F32), rstd=A.alloc([T], F32),
                           tmp=tmp8, hT=A.alloc([NCH, T], BF16), cqb=A.alloc([3, T], BF16)))
        wdq = A.alloc([8, QR], BF16)
        P.dma(POOL, lambda e: e.dma_start(out=wdq, in_=w_dq[jl].rearrange("(kc p) n -> p kc n", p=128)),
              writes=["wdq"], key="wdq")
        mpieces = mods_pieces(3, ps[7], "ps7", None, 256) if l == 2 else []
        for tb in range(NTB):
            for _ in range(4):
                if mpieces:
                    mpieces.pop(0)()
            t0, t1 = tb * T, (tb + 1) * T
            par = tb % 2
            tg = f"_{par}"
            b = B2[par]
            sq, rs, rstd, tmp, hT, cqb = b["sq"], b["rs"], b["rstd"], b["tmp"], b["hT"], b["cqb"]
            pb = 3 * par
            rstd_block([xT[:, c, t0:t1] for c in range(NCH)], D, sq, ps[6], "ps6", rs, rstd, "rstd" + tg,
                       [xtok(c, t0, t1) for c in range(NCH)], "")
            for c in range(NCH):
                P.dve(lambda e, c=c, t0=t0, t1=t1, tmp=tmp, rstd=rstd: e.tensor_tensor(
                    tmp[c], xT[:, c, t0:t1], rstd, ALU.mult),
                    reads=xtok(c, t0, t1) + ["rstd" + tg], writes=[f"tmp{c}"])
                P.act(lambda e, c=c, tmp=tmp, hT=hT: e.activation(hT[:, c, :], tmp[c], AF.Identity,
                                                                  bias=mcol(l, 0, c), scale=acol(l, 0, c)),
                      reads=[f"tmp{c}", "mods", "coef"], writes=[f"hT{c}{tg}"])
            for mi in range(3):
                for kc in range(8):
                    P.pe(lambda e, mi=mi, kc=kc, hT=hT, pb=pb: e.matmul(
                        ps[pb + mi][:], wdq[:, kc, mi * 128:(mi + 1) * 128], hT[:, kc, :],
                        start=(kc == 0), stop=(kc == 7)),
                        reads=["wdq", f"hT{kc}{tg}"], writes=[f"ps{pb + mi}"])
            rstd_block([ps[pb][:], ps[pb + 1][:], ps[pb + 2][:]], QR, sq, ps[6], "ps6", rs, rstd, "rstd" + tg,
                       [[f"ps{pb}"], [f"ps{pb + 1}"], [f"ps{pb + 2}"]], "")
            for mi in range(3):
                P.dve(lambda e, mi=mi, cqb=cqb, rstd=rstd, pb=pb: e.scalar_tensor_tensor(
                    cqb[:, mi, :], ps[pb + mi][:], vc("qng", jl * 3 + mi), rstd, ALU.mult, ALU.mult),
                    reads=[f"ps{pb + mi}", "rstd" + tg, "vecs"], writes=["cqb" + tg])
            P.dma(SP, lambda e, t0=t0, t1=t1, cqb=cqb: e.dma_start(out=cq_d[:, :, t0:t1], in_=cqb),
                  reads=["cqb" + tg], writes=[f"cq_d{tb}"], key="cqo" + tg)
        while mpieces:
            mpieces.pop(0)()
        P.barrier()
        A.reset(m)

    def mla_attn_phase(l):
        jl = l - 2
        m = A.mark()
        qn = A.alloc([S], BF16)
        qra = A.alloc([S], BF16)
        kn = A.alloc([S], BF16)
        kra = A.alloc([S], BF16)
        vsb = A.alloc([S // 128, 128], BF16)
        og = [A.alloc([T], BF16) for _ in range(2)]
        cqbs = [A.alloc([3, T], BF16) for _ in range(2)]
        ckvbs = [A.alloc([2, T], BF16) for _ in range(2)]
        csb = A.alloc([2, T], F32)
        R1 = A.alloc([T], F32)
        R2 = A.alloc([3 * T], BF16)
        t1_ = R1[0:64, :]
        rl = R1
        t2_ = R2[:, 0:2 * T].bitcast(F32)[0:64, :]
        PT = [R2[:, 0:T], R2[:, T:2 * T], R2[:, 2 * T:3 * T]]
        wq = A.alloc([3, 256], BF16)
        wk = A.alloc([2, 128], BF16)
        wv = A.alloc([2, 128], BF16)
        maskg = A.alloc([4, T], BF16)
        mx = [A.alloc([8], F32) for _ in range(2)]
        mrow = [A.alloc([2], F32) for _ in range(2)]
        negm65 = [A.alloc([66], BF16) for _ in range(2)]
        P.dma(SP, lambda e: e.dma_start(out=kra[0:64, :], in_=kr_d), writes=["kr"], key="krl")
        P.dve(lambda e: e.memset(kra[64:65, :], 1.0), writes=["kr1"])
        for par in range(2):
            P.dve(lambda e, par=par: e.memset(negm65[par], 0.0), writes=[f"negm{par}"])
        P.dma(POOL, lambda e: e.dma_start(out=maskg, in_=dr["maskg"].rearrange("p (r q) -> p r q", r=4)),
              writes=["maskg"], key="maskg")
        wq_src = w_uq[jl].rearrange("(kc p) (h n) -> p kc h n", p=128, n=192)
        wk_src = w_uk.rearrange("(kc p) (h n) -> p kc h n", p=128, n=128)
        wv_src = w_uv.rearrange("(kc p) (h n) -> p kc h n", p=128, n=128)
        NQ = S // 128
        for h in range(NH):
            P.dma(POOL, lambda e, h=h: e.dma_start(out=wq[:, :, 0:192], in_=wq_src[:, :, h, :]),
                  writes=["wq"], key="wq")
            P.dma(POOL, lambda e, h=h: e.dma_start(out=wq[:, :, 192:224], in_=wq_src[:, :, h, 160:192]),
                  writes=["wq"], key="wq")
            P.dma(POOL, lambda e, h=h: e.dma_start(out=wq[:, :, 224:256], in_=wq_src[:, :, h, 128:160]),
                  writes=["wq"], key="wq")
            P.dma(POOL, lambda e, h=h: e.dma_start(out=wk, in_=wk_src[:, :, h, :]), writes=["wk"], key="wk")
            P.dma(POOL, lambda e, h=h: e.dma_start(out=wv, in_=wv_src[:, :, h, :]), writes=["wv"], key="wv")
            for tb in range(NTB):
                t0, t1 = tb * T, (tb + 1) * T
                cqb, ckvb = cqbs[tb % 2], ckvbs[tb % 2]
                cqtok, ckvtok = f"cqb{tb % 2}", f"ckvb{tb % 2}"
                P.dma(SP, lambda e, t0=t0, t1=t1, cqb=cqb: e.dma_start(out=cqb, in_=cq_d[:, :, t0:t1]),
                      writes=[cqtok], key=cqtok)
                P.dma(SP, lambda e, t0=t0, t1=t1, ckvb=ckvb: e.dma_start(out=ckvb, in_=ckv_d[:, :, t0:t1]),
                      writes=[ckvtok], key=ckvtok)
                P.dma(SP, lambda e, t0=t0, t1=t1: e.dma_start(
                    out=csb[0:64], in_=cs_d[:, :, t0:t1].rearrange("w p t -> p w t")), writes=["csb"], key="csb")
                for (bank, btok, c0, c1, mm) in [(ps[0], "ps0", 0, 128, 128), (ps[1], "ps1", 128, 192, 64),
                                                  (ps[2], "ps2", 192, 256, 64)]:
                    for kc in range(3):
                        P.pe(lambda e, bank=bank, c0=c0, c1=c1, mm=mm, kc=kc, cqb=cqb: e.matmul(
                            bank[0:mm, :], wq[:, kc, c0:c1], cqb[:, kc, :], start=(kc == 0), stop=(kc == 2)),
                            reads=["wq", cqtok], writes=[btok])
                for kc in range(2):
                    P.pe(lambda e, kc=kc, ckvb=ckvb: e.matmul(ps[3][:], wk[:, kc, :], ckvb[:, kc, :],
                                                   start=(kc == 0), stop=(kc == 1)),
                         reads=["wk", ckvtok], writes=["ps3"])
                P.act(lambda e, t0=t0, t1=t1: e.activation(qn[:, t0:t1], ps[0][:], AF.Identity, scale=QSCALE),
                      reads=["ps0"], writes=[f"qn{tb}"])
                P.act(lambda e, t0=t0, t1=t1: e.activation(kn[:, t0:t1], ps[3][:], AF.Copy),
                      reads=["ps3"], writes=[f"kn{tb}"])
                P.dve(lambda e: e.scalar_tensor_tensor(t1_, ps[1][0:64, :], QSCALE, csb[0:64, 0, :], ALU.mult, ALU.mult),
                      reads=["ps1", "csb"], writes=["R1"])
                P.dve(lambda e: e.scalar_tensor_tensor(t2_, ps[2][0:64, :], QSCALE, csb[0:64, 1, :], ALU.mult, ALU.mult),
                      reads=["ps2", "csb"], writes=["PT0", "PT1"])
                P.dve(lambda e, t0=t0, t1=t1: e.tensor_tensor(qra[0:64, t0:t1], t1_, t2_, ALU.add),
                      reads=["R1", "PT0", "PT1"], writes=[f"qr{tb}"])
                for i in range(4):
                    ti = tb * 4 + i
                    vb = ps[4 + i % 2]
                    vtok = f"ps{4 + i % 2}"
                    for kc in range(2):
                        P.pe(lambda e, vb=vb, kc=kc, i=i, ckvb=ckvb: e.matmul(
                            vb[:, 0:128], ckvb[:, kc, i * 128:(i + 1) * 128], wv[:, kc, :],
                            start=(kc == 0), stop=(kc == 1)), reads=["wv", ckvtok], writes=[vtok])
                    P.act(lambda e, vb=vb, ti=ti: e.activation(vsb[:, ti, :], vb[:, 0:128], AF.Copy),
                          reads=[vtok], writes=[f"v{ti}"])
            allk = [f"kn{tb}" for tb in range(NTB)] + ["kr", "kr1"]
            P.dve(lambda e: e.memset(qra[64:65, :], 0.0), writes=[f"qm{i}" for i in range(NQ)])
            p1n = [0]

            def pass1_chunks(i):
                par = i % 2
                G = i // 4
                nk = (i + 1) * 128
                nchunk = (nk + 511) // 512
                qtok = [f"qn{G}", f"qr{G}", f"qm{i}"]
                out = []
                for cidx in range(nchunk):
                    def chunk(cidx=cidx):
                        k0 = cidx * 512
                        k1 = min(nk, k0 + 512)
                        bank = ps[p1n[0] % 2]
                        btok = f"ps{p1n[0] % 2}"
                        p1n[0] += 1
                        isdiag = (cidx == nchunk - 1)
                        P.pe(lambda e: e.matmul(
                            bank[:, 0:k1 - k0], qn[:, i * 128:(i + 1) * 128], kn[:, k0:k1], start=True, stop=False),
                            reads=qtok + allk, writes=[btok])
                        P.pe(lambda e: e.matmul(
                            bank[:, 0:k1 - k0], qra[0:65, i * 128:(i + 1) * 128], kra[0:65, k0:k1], start=False,
                            stop=(not isdiag)), reads=qtok + allk, writes=[btok])
                        if isdiag:
                            P.pe(lambda e: e.matmul(
                                bank[:, k1 - k0 - 128:k1 - k0], identb, maskqb, start=False, stop=True),
                                reads=["identb", "maskqb"], writes=[btok])
                        P.dve(lambda e: e.reduce_max(mx[par][:, cidx:cidx + 1], bank[:, 0:k1 - k0], AX.X),
                              reads=[btok], writes=[f"mx{par}"])
                        if isdiag:
                            P.dve(lambda e: e.reduce_max(mrow[par][:, 0:1], mx[par][:, 0:nchunk], AX.X),
                                  reads=[f"mx{par}"], writes=[f"mrow{par}"])
                            P.dve(lambda e: e.tensor_scalar(negm65[par][:, 64:65], mrow[par][:, 0:1], -1.0, None,
                                                            ALU.mult),
                                  reads=[f"mrow{par}"], writes=[f"negm{par}"])
                    out.append(chunk)

                def tail():
                    P.pe(lambda e: e.matmul(ps[7][0:65, par * 128:(par + 1) * 128], negm65[par][:, 0:65],
                                            identb, start=True, stop=True),
                         reads=[f"negm{par}", "identb"], writes=["ps7"])
                    P.act(lambda e: e.activation(qra[64:65, i * 128:(i + 1) * 128],
                                                 ps[7][64:65, par * 128:(par + 1) * 128], AF.Copy),
                          reads=["ps7"], writes=[f"qm{i}"])
                return out, tail

            stn = [0]

            def pass2(G, fillers):
                g0, g1 = G * T, (G + 1) * T
                nkb = 4 * G + 4
                OT, ottok = ps[4], "ps4"
                Lb = ps[5]
                qtok = [f"qn{G}", f"qr{G}"] + [f"qm{4 * G + q}" for q in range(4)]
                pend = []
                for j in range(nkb):
                    sidx = stn[0] % 3
                    stn[0] += 1
                    bi = [2, 3, 6][sidx]
                    bank = ps[bi]
                    btok = f"ps{bi}"
                    pt = PT[sidx]
                    pttok = f"PT{sidx}"
                    P.pe(lambda e, bank=bank, j=j: e.matmul(
                        bank[:], kn[:, j * 128:(j + 1) * 128], qn[:, g0:g1], start=True, stop=False),
                        reads=qtok + allk, writes=[btok])
                    P.pe(lambda e, bank=bank, j=j: e.matmul(
                        bank[:], kra[0:65, j * 128:(j + 1) * 128], qra[0:65, g0:g1], start=False,
                        stop=(j < 4 * G)), reads=qtok + allk, writes=[btok])
                    if j >= 4 * G:
                        P.pe(lambda e, bank=bank, j=j: e.matmul(
                            bank[:], identb, maskg[:, j - 4 * G, :], start=False, stop=True),
                            reads=["identb", "maskg"], writes=[btok])
                    P.act(lambda e, pt=pt, bank=bank: e.activation(pt, bank[:], AF.Exp),
                          reads=[btok], writes=[pttok])

                    def pv(j=j, pt=pt, pttok=pttok):
                        P.pe(lambda e: e.matmul(OT[:], vsb[:, j, :], pt, start=(j == 0), stop=(j == nkb - 1)),
                             reads=[pttok, f"v{j}"], writes=[ottok])
                        P.pe(lambda e: e.matmul(Lb[:], onesb, pt, start=(j == 0), stop=(j == nkb - 1)),
                             reads=[pttok, "onesb"], writes=["ps5"])
                    pend.append(pv)
                    if len(pend) > 2:
                        pend.pop(0)()
                    if fillers:
                        fillers.pop(0)()
                for pv_ in pend:
                    pv_()
                while fillers:
                    fillers.pop(0)()
                P.act(lambda e: e.activation(rl, Lb[:], AF.Ln), reads=["ps5"], writes=["R1"])
                P.act(lambda e: e.activation(rl, rl, AF.Exp, scale=-1.0), reads=["R1"], writes=["R1"])
                ogb, ogtok = og[G % 2], f"og{G % 2}"
                P.dve(lambda e: e.tensor_tensor(ogb, OT[:], rl, ALU.mult),
                      reads=[ottok, "R1"], writes=[ogtok])
                P.dma(SP, lambda e, h=h: e.dma_start(out=oT_d[:, h, g0:g1], in_=ogb),
                      reads=[ogtok], writes=[f"oT_d{G}"], key=ogtok)

            def group_fillers(G):
                fl = []
                tails = []
                for i in range(4 * G, 4 * G + 4):
                    chunks, tail = pass1_chunks(i)
                    for ch in chunks:
                        fl.append(ch)
                        tails = [(tl, n - 1) for tl, n in tails]
                        while tails and tails[0][1] <= 0:
                            fl.append(tails.pop(0)[0])
                    tails.append((tail, 2))
                fl += [tl for tl, _ in tails]
                return fl

            for f_ in group_fillers(0):
                f_()
            for G in range(NTB):
                fillers = group_fillers(G + 1) if G + 1 < NTB else []
                pass2(G, fillers)
        P.barrier()
        A.reset(m)

    def mla_o_phase(l):
        jl = l - 2
        m = A.mark()
        wo = A.alloc([NH, D], BF16)
        ob = [A.alloc([NH, T], BF16) for _ in range(2)]
        P.dma(POOL, lambda e: e.dma_start(out=wo, in_=w_o[jl].rearrange("(h p) n -> p h n", p=128)),
              writes=["wo"], key="wo")
        ny = 0
        for tb in range(NTB):
            t0, t1 = tb * T, (tb + 1) * T
            o_ = ob[tb % 2]
            otok = f"ob{tb % 2}"
            P.dma(SP, lambda e, o_=o_, t0=t0, t1=t1: e.dma_start(out=o_, in_=oT_d[:, :, t0:t1]),
                  writes=[otok], key=otok)
            for c in range(NCH):
                yb = ps[ny % 2]
                ytok = f"ps{ny % 2}"
                ny += 1
                for h in range(NH):
                    P.pe(lambda e, yb=yb, o_=o_, h=h, c=c: e.matmul(
                        yb[:], wo[:, h, c * 128:(c + 1) * 128], o_[:, h, :], start=(h == 0), stop=(h == NH - 1)),
                        reads=["wo", otok], writes=[ytok])
                P.dve(lambda e, yb=yb, c=c, t0=t0, t1=t1: e.scalar_tensor_tensor(
                    xT[:, c, t0:t1], yb[:], mcol(l, 2, c), xT[:, c, t0:t1], ALU.mult, ALU.add),
                    reads=[ytok, "mods"] + xtok(c, t0, t1), writes=xtok(c, t0, t1))
        P.barrier()
        A.reset(m)

    phases = []
    for l in range(DEPTH):
        if l < 2:
            phases.append((f"mix{l}", lambda l=l: pool_phase(l)))
        else:
            phases.append((f"q{l}", lambda l=l: mla_q_phase(l)))
            phases.append((f"attn{l}", lambda l=l: mla_attn_phase(l)))
            phases.append((f"mix{l}", lambda l=l: mla_o_phase(l)))
        phases.append((f"ffn{l}", lambda l=l: ffn_phase(l)))
        if l == 1:
            phases.append(("kv", kv_phase))
    if stop_after == "init":
        dump_x(False)
        return finish()
    for tag, fn in phases:
        fn()
        if stop_after == tag:
            dump_x(False)
            return finish()
    dump_x(True)
    return finish()


_PROG = {}


def _in_maps(inputs):
    cst = _consts()
    mg = _maskg()
    maps = []
    f32 = lambda a: np.ascontiguousarray(np.asarray(a, np.float32))
    shared = {k: f32(inputs[k]) for k in ["mod_w", "pool_w", "w_dkv", "w_uk", "w_uv", "w_dq", "w_uq", "w_o",
                                          "w_up", "w_down"]}
    x = np.asarray(inputs["x"], np.float32)
    pos = np.asarray(inputs["positions"], np.int32)
    for b in range(8):
        mp = dict(shared)
        mp["x"] = np.ascontiguousarray(x[b])
        mp["pos"] = np.ascontiguousarray(pos[b:b + 1])
        mp["vecs"] = _build_vecs(inputs, b)
        mp["cst"] = cst
        mp["maskg"] = mg
        maps.append(mp)
    return maps


def kernel(**inputs):
    if "full" not in _PROG:
        _PROG["full"] = build_program(None)
    nc = _PROG["full"]
    res = run_bass_kernel_spmd(nc, _in_maps(inputs), core_ids=list(range(8)))
    return np.stack([np.asarray(r["y"], np.float32) for r in res.results], axis=0)
```

```python
import contextlib
import numpy as np
import concourse.bass as bass
import concourse.mybir as mybir
from concourse.bass_utils import run_bass_kernel_spmd

F32 = mybir.dt.float32
BF16 = mybir.dt.bfloat16
I32 = mybir.dt.int32
AF = mybir.ActivationFunctionType
ALU = mybir.AluOpType
AX = mybir.AxisListType

PE, ACT, DVE, POOL, SP = "pe", "act", "dve", "pool", "sp"
ENGS = [PE, ACT, DVE, POOL, SP]

D = 1024
S = 4096
DEPTH = 4
NCH = 8
T = 512
NTB = S // T
FF = 2816
NF = FF // 128
QR = 384
KVR = 256
NH = 8
ROPE = 64
WINDOWS = (2, 4, 8, 16)
EPS = 1e-6
NEG = -30000.0
QSCALE = 192.0 ** -0.5


class _Op:
    __slots__ = ("eng", "fn", "deps", "is_dma", "key", "sig", "needed")

    def __init__(self, eng, fn, is_dma, key):
        self.eng = eng
        self.fn = fn
        self.deps = []
        self.is_dma = is_dma
        self.key = key
        self.sig = None
        self.needed = False


class Prog:
    def __init__(self, nc):
        self.nc = nc
        self.ops = {e: [] for e in ENGS}
        self.last_w = {}
        self.readers = {}
        self.dma_cnt = {}
        self.pending_dma = []

    def _record(self, eng, fn, reads, writes, is_dma=False, key=None):
        op = _Op(eng, fn, is_dma, key)
        deps = {}
        for t in reads:
            w = self.last_w.get(t)
            if w is not None:
                deps[id(w)] = w
        for t in writes:
            w = self.last_w.get(t)
            if w is not None:
                deps[id(w)] = w
            for r in self.readers.get(t, {}).values():
                deps[id(r)] = r
        for d in deps.values():
            if d is op:
                continue
            if (not d.is_dma) and (not is_dma) and d.eng == PE and eng == PE:
                continue
            op.deps.append(d)
            d.needed = True
        for t in writes:
            self.last_w[t] = op
            self.readers[t] = {}
        rk = (eng, id(op)) if is_dma else eng
        for t in reads:
            if t not in writes:
                self.readers.setdefault(t, {})[rk] = op
        self.ops[eng].append(op)
        if is_dma:
            self.pending_dma.append(op)
        return op

    def pe(self, fn, reads=(), writes=()):
        return self._record(PE, fn, reads, writes)

    def act(self, fn, reads=(), writes=()):
        return self._record(ACT, fn, reads, writes)

    def dve(self, fn, reads=(), writes=()):
        return self._record(DVE, fn, reads, writes)

    def pool(self, fn, reads=(), writes=()):
        return self._record(POOL, fn, reads, writes)

    def on(self, eng, fn, reads=(), writes=()):
        return self._record(eng, fn, reads, writes)

    def dma(self, eng, fn, reads=(), writes=(), key=None):
        return self._record(eng, fn, reads, writes, is_dma=True, key=key)

    def barrier(self):
        lasts = []
        for e in ENGS:
            for op in reversed(self.ops[e]):
                if op.fn is not None and not op.is_dma:
                    lasts.append(op)
                    break
        dmas = list(self.pending_dma)
        self.pending_dma = []
        for e in ENGS:
            b = _Op(e, None, False, None)
            for d in lasts + dmas:
                if (not d.is_dma) and d.eng == e and e == PE:
                    continue
                b.deps.append(d)
                d.needed = True
            self.ops[e].append(b)
        self.last_w = {}
        self.readers = {}

    def emit(self):
        nc = self.nc
        for e in ENGS:
            cnt = 0
            for op in self.ops[e]:
                if op.is_dma:
                    c = self.dma_cnt.get(op.key, 0) + 16
                    self.dma_cnt[op.key] = c
                    op.sig = ("dma_" + op.key, c)
                elif op.needed and op.fn is not None:
                    cnt += 1
                    op.sig = ("eng_" + e, cnt)
        names = ["eng_" + e for e in ENGS] + ["dma_" + k for k in self.dma_cnt]
        with contextlib.ExitStack() as st:
            sems = {n: st.enter_context(nc.semaphore(n)) for n in names}
            block = st.enter_context(nc.Block())

            def make(e):
                def body(h):
                    seen = {}
                    for op in self.ops[e]:
                        for d in op.deps:
                            sn, v = d.sig
                            if seen.get(sn, 0) >= v:
                                continue
                            seen[sn] = v
                            h.wait_ge(sems[sn], v)
                        if op.fn is None:
                            continue
                        ins = op.fn(h)
                        if op.sig is not None:
                            ins.then_inc(sems[op.sig[0]], 16 if op.is_dma else 1)
                return body

            block.tensor(make(PE))
            block.scalar(make(ACT))
            block.vector(make(DVE))
            block.gpsimd(make(POOL))
            block.sync(make(SP))


class Arena:
    def __init__(self, nc, nbytes):
        self.t = nc.alloc_sbuf_tensor("arena", [128, nbytes // 2], BF16)
        self.nbytes = nbytes
        self.off = 0

    def alloc(self, shape, dtype, parts=128):
        esz = 4 if dtype in (F32, I32) else 2
        n = 1
        for s in shape:
            n *= s
        nb = (n * esz + 63) // 64 * 64
        assert self.off + nb <= self.nbytes, f"arena overflow {self.off}+{nb}>{self.nbytes}"
        v = self.t[0:parts, self.off // 2:(self.off + n * esz) // 2]
        if esz == 4:
            v = v.bitcast(dtype)
        self.off += nb
        if len(shape) == 2:
            v = v.rearrange("p (a b) -> p a b", a=shape[0])
        elif len(shape) == 3:
            v = v.rearrange("p (a b c) -> p a b c", a=shape[0], b=shape[1])
        return v

    def mark(self):
        return self.off

    def reset(self, m):
        self.off = m


def _vec_layout():
    cols = {}
    off = 0
    for name, n in [("cT", 8), ("mod_b", 4 * 48), ("n1g", 32), ("n2g", 32), ("pool_b", 16),
                    ("pool_s", 16), ("kvg", 8), ("ckvg", 2), ("qng", 6), ("convw", 4 * 3 * NF),
                    ("convb", 4 * NF), ("fg", 8), ("eps", 1), ("invf", 1), ("sgn", 1),
                    ("invc", 64), ("one", 1)]:
        cols[name] = off
        off += n
    return cols, off


VCOL, NV = _vec_layout()


def _chunks(v):
    v = np.asarray(v, np.float32)
    lead = v.shape[:-1]
    n = v.shape[-1] // 128
    v = v.reshape(lead + (n, 128))
    v = np.moveaxis(v, -1, 0)
    return np.ascontiguousarray(v.reshape(128, -1))


def _build_vecs(inp, b):
    V = np.zeros((128, NV), np.float32)

    def put(name, arr):
        V[:, VCOL[name]:VCOL[name] + arr.shape[1]] = arr

    put("cT", _chunks(inp["c"][b]))
    put("mod_b", _chunks(inp["mod_b"]))
    put("n1g", _chunks(inp["norm1_g"]))
    put("n2g", _chunks(inp["norm2_g"]))
    put("pool_b", _chunks(inp["pool_b"]))
    put("pool_s", _chunks(inp["pool_scale"]))
    put("kvg", _chunks(inp["kv_in_g"]))
    put("ckvg", _chunks(inp["ckv_norm_g"]))
    put("qng", _chunks(inp["q_norm_g"]))
    put("convw", _chunks(inp["conv_w"]))
    put("convb", _chunks(inp["conv_b"]))
    put("fg", _chunks(inp["final_g"]))
    V[:, VCOL["eps"]] = EPS
    V[:, VCOL["one"]] = 1.0
    inv = (1.0 / (10000.0 ** (np.arange(0, ROPE, 2, dtype=np.float32) / ROPE))).astype(np.float32)
    V[0:32, VCOL["invf"]] = inv
    V[32:64, VCOL["invf"]] = inv
    V[0:32, VCOL["sgn"]] = -1.0
    V[32:64, VCOL["sgn"]] = 1.0
    for g, w in enumerate(WINDOWS):
        for t in range(16):
            V[:, VCOL["invc"] + g * 16 + t] = 1.0 / min(t + 1, w)
    return V


def _consts():
    ident = np.eye(128, dtype=np.float32)
    q = np.arange(128)[:, None]
    k = np.arange(128)[None, :]
    maskq = np.where(k <= q, 0.0, NEG).astype(np.float32)
    maskt = np.ascontiguousarray(maskq.T)
    return np.concatenate([ident, maskq, maskt], axis=1)


def _maskg():
    mg = np.zeros((128, 4, 4, 128), np.float32)
    k = np.arange(128)[:, None]
    q = np.arange(128)[None, :]
    tri = np.where(k <= q, 0.0, NEG).astype(np.float32)
    for r in range(4):
        for ip in range(4):
            if ip < r:
                mg[:, r, ip, :] = NEG
            elif ip == r:
                mg[:, r, ip, :] = tri
    return np.ascontiguousarray(mg.reshape(128, 4 * 512))


def build_program(stop_after=None):
    nc = bass.Bass("TRN2", target_bir_lowering=False)
    dr = {}

    def din(name, shape, dt=F32):
        dr[name] = nc.dram_tensor(name, list(shape), dt, kind="ExternalInput").ap()
        return dr[name]

    x_d = din("x", [S, D])
    pos_d = din("pos", [1, S], I32)
    vecs_d = din("vecs", [128, NV])
    cst_d = din("cst", [128, 384])
    din("maskg", [128, 4 * T])
    mod_w = din("mod_w", [DEPTH, D, 6 * D])
    pool_w = din("pool_w", [2, 4, 256, 256])
    w_dkv = din("w_dkv", [D, KVR + ROPE])
    w_uk = din("w_uk", [KVR, NH * 128])
    w_uv = din("w_uv", [KVR, NH * 128])
    w_dq = din("w_dq", [2, D, QR])
    w_uq = din("w_uq", [2, QR, NH * 192])
    w_o = din("w_o", [2, NH * 128, D])
    w_up = din("w_up", [DEPTH, D, 2 * FF])
    w_down = din("w_down", [DEPTH, FF, D])
    y_d = nc.dram_tensor("y", [S, D], F32, kind="ExternalOutput").ap()
    cs_d = nc.dram_tensor("cs_scr", [2, 64, S], F32).ap()
    ckv_d = nc.dram_tensor("ckv_scr", [128, 2, S], BF16).ap()
    kr_d = nc.dram_tensor("kr_scr", [64, S], BF16).ap()
    cq_d = nc.dram_tensor("cq_scr", [128, 3, S], BF16).ap()
    oT_d = nc.dram_tensor("oT_scr", [128, NH, S], BF16).ap()

    A = Arena(nc, 211200)
    P = Prog(nc)
    ps = [nc.alloc_psum_tensor(f"ps{i}", [128, 512], F32) for i in range(8)]

    xT = A.alloc([NCH, S], F32)
    vecs = A.alloc([NV], F32)
    cst = A.alloc([384], F32)
    mods = A.alloc([DEPTH * 48], F32)
    coef = A.alloc([DEPTH * 2 * 8], F32)
    pcoef = A.alloc([2 * 2 * 8], F32)
    identb = A.alloc([128], BF16)
    onesb = A.alloc([128], BF16)
    maskqb = A.alloc([128], BF16)
    masktb = A.alloc([128], BF16)
    siluc = A.alloc([8], BF16)
    identf = cst[:, 0:128]
    PH = A.mark()

    def vc(name, i=0, n=1, parts=128):
        o = VCOL[name] + i
        return vecs[0:parts, o:o + n]

    def mcol(l, k, c):
        o = l * 48 + k * 8 + c
        return mods[:, o:o + 1]

    def acol(l, which, c):
        o = (l * 2 + which) * 8 + c
        return coef[:, o:o + 1]

    def xtok(c, t0, t1):
        return [f"x{c}_{b}" for b in range(t0 // T, (t1 + T - 1) // T)]

    eps_col = vc("eps")

    P.dma(SP, lambda e: e.dma_start(out=vecs, in_=vecs_d), writes=["vecs"], key="c0")
    P.dma(SP, lambda e: e.dma_start(out=cst, in_=cst_d), writes=["cst"], key="c1")
    P.dve(lambda e: e.tensor_copy(identb, cst[:, 0:128]), reads=["cst"], writes=["identb"])
    P.dve(lambda e: e.tensor_copy(maskqb, cst[:, 128:256]), reads=["cst"], writes=["maskqb"])
    P.dve(lambda e: e.tensor_copy(masktb, cst[:, 256:384]), reads=["cst"], writes=["masktb"])
    P.dve(lambda e: e.memset(onesb, 1.0), writes=["onesb"])
    P.act(lambda e: e.activation(siluc, vc("cT", 0, 8), AF.Silu), reads=["vecs"], writes=["siluc"])

    xin = [A.alloc([4, D], F32) for _ in range(2)]
    for tg in range(NTB):
        xb = xin[tg % 2]
        P.dma(SP, lambda e, xb=xb, tg=tg: e.dma_start(
            out=xb, in_=x_d[tg * T:(tg + 1) * T, :].rearrange("(i p) d -> p i d", p=128)),
            writes=[f"xin{tg % 2}"], key=f"xin{tg % 2}")
        for c in range(NCH):
            bank = ps[c % 4]
            for i in range(4):
                P.pe(lambda e, bank=bank, xb=xb, i=i, c=c: e.transpose(
                    bank[:, i * 128:(i + 1) * 128], xb[:, i, c * 128:(c + 1) * 128], identf),
                    reads=[f"xin{tg % 2}", "cst"], writes=[f"ps{c % 4}"])
            dst = xT[:, c, tg * T:(tg + 1) * T]
            if c % 2 == 0:
                P.dve(lambda e, dst=dst, bank=bank: e.tensor_copy(dst, bank[:]),
                      reads=[f"ps{c % 4}"], writes=xtok(c, tg * T, (tg + 1) * T))
            else:
                P.act(lambda e, dst=dst, bank=bank: e.activation(dst, bank[:], AF.Copy),
                      reads=[f"ps{c % 4}"], writes=xtok(c, tg * T, (tg + 1) * T))

    def mods_pieces(l, bank, btok, mw=None, wcols=512):
        if mw is None:
            mw = [A.alloc([8, wcols], BF16) for _ in range(2)]
        npc = wcols // 128
        out = []
        for q in range(6 * D // wcols):
            def piece(q=q):
                buf = mw[q % 2]
                mtok = f"mw{q % 2}"
                P.dma(POOL, lambda e: e.dma_start(
                    out=buf, in_=mod_w[l][:, q * wcols:(q + 1) * wcols].rearrange("(kc p) n -> p kc n", p=128)),
                    writes=[mtok], key=mtok)
                for jj in range(npc):
                    j = q * npc + jj
                    for kc in range(8):
                        P.pe(lambda e, jj=jj, kc=kc, j=j: e.matmul(
                            bank[:, j:j + 1], buf[:, kc, jj * 128:(jj + 1) * 128], siluc[:, kc:kc + 1],
                            start=(kc == 0), stop=(kc == 7)),
                            reads=[mtok, "siluc"], writes=[btok])
            out.append(piece)

        def fin():
            P.dve(lambda e: e.tensor_tensor(mods[:, l * 48:(l + 1) * 48], bank[:, 0:48], vc("mod_b", l * 48, 48),
                                            ALU.add), reads=[btok, "vecs"], writes=["mods"])
            for which, (gname, k) in enumerate([("n1g", 1), ("n2g", 4)]):
                dst = coef[:, (l * 2 + which) * 8:(l * 2 + which) * 8 + 8]
                src = mods[:, l * 48 + k * 8:l * 48 + k * 8 + 8]
                P.dve(lambda e, dst=dst, src=src, gname=gname: e.scalar_tensor_tensor(
                    dst, src, 1.0, vc(gname, l * 8, 8), ALU.add, ALU.mult),
                    reads=["mods", "vecs"], writes=["coef"])
            if l < 2:
                pc = pcoef[:, (l * 2) * 8:(l * 2) * 8 + 8]
                pb = pcoef[:, (l * 2 + 1) * 8:(l * 2 + 1) * 8 + 8]
                g1 = mods[:, l * 48 + 16:l * 48 + 24]
                P.dve(lambda e: e.tensor_tensor(pc, g1, vc("pool_s", l * 8, 8), ALU.mult),
                      reads=["mods", "vecs"], writes=["pcoef"])
                P.dve(lambda e: e.tensor_tensor(pb, pc, vc("pool_b", l * 8, 8), ALU.mult),
                      reads=["pcoef", "vecs"], writes=["pcoef"])
        out.append(fin)
        return out

    def mods_ops(l, bank, btok, mw=None, wcols=512):
        for f_ in mods_pieces(l, bank, btok, mw, wcols):
            f_()

    mw_init = [A.alloc([8, 512], BF16) for _ in range(2)]
    mods_ops(0, ps[5], "ps5", mw_init)
    mods_ops(1, ps[4], "ps4", mw_init)

    HS = S // 8
    posf = A.alloc([HS], F32, parts=64)
    ang = A.alloc([HS], F32, parts=64)
    kf = A.alloc([HS], F32, parts=64)
    ki = A.alloc([HS], I32, parts=64)
    rr = A.alloc([HS], F32, parts=64)
    C1 = 6.28125
    C2 = float(2 * np.pi - 6.28125)
    for hf in range(S // HS):
        P.dma(SP, lambda e, hf=hf: e.dma_start(
            out=ki, in_=pos_d[:, hf * HS:(hf + 1) * HS].partition_broadcast(64)),
            writes=["ki"], key="pos")
        P.dve(lambda e: e.tensor_copy(posf, ki), reads=["ki"], writes=["posf"])
        P.dve(lambda e: e.tensor_scalar(ang, posf, vc("invf", 0, 1, 64), None, ALU.mult),
              reads=["posf", "vecs"], writes=["ang"])
        for which, shift in enumerate([float(np.pi / 2), 0.0]):
            P.dve(lambda e, shift=shift: e.tensor_scalar(rr, ang, shift, None, ALU.add),
                  reads=["ang"], writes=["rr"])
            P.dve(lambda e: e.tensor_scalar(kf, rr, float(1 / (2 * np.pi)), None, ALU.mult),
                  reads=["rr"], writes=["kf"])
            P.dve(lambda e: e.tensor_copy(ki, kf), reads=["kf"], writes=["ki"])
            P.dve(lambda e: e.tensor_copy(kf, ki), reads=["ki"], writes=["kf"])
            P.dve(lambda e: e.scalar_tensor_tensor(rr, kf, -C1, rr, ALU.mult, ALU.add),
                  reads=["kf", "rr"], writes=["rr"])
            P.dve(lambda e: e.scalar_tensor_tensor(rr, kf, -C2, rr, ALU.mult, ALU.add),
                  reads=["kf", "rr"], writes=["rr"])
            P.dve(lambda e: e.tensor_scalar(kf, rr, float(np.pi), float(-2 * np.pi), ALU.is_gt, ALU.mult),
                  reads=["rr"], writes=["kf"])
            P.dve(lambda e: e.tensor_tensor(rr, rr, kf, ALU.add), reads=["kf", "rr"], writes=["rr"])
            P.dve(lambda e: e.tensor_scalar(kf, rr, float(-np.pi), float(2 * np.pi), ALU.is_lt, ALU.mult),
                  reads=["rr"], writes=["kf"])
            P.dve(lambda e: e.tensor_tensor(rr, rr, kf, ALU.add), reads=["kf", "rr"], writes=["rr"])
            if which == 0:
                P.act(lambda e: e.activation(rr, rr, AF.Sin), reads=["rr"], writes=["rr"])
            else:
                P.act(lambda e: e.activation(rr, rr, AF.Sin, scale=vc("sgn", 0, 1, 64)), reads=["rr", "vecs"],
                      writes=["rr"])
            P.dma(SP, lambda e, which=which, hf=hf: e.dma_start(
                out=cs_d[which, :, hf * HS:(hf + 1) * HS], in_=rr), reads=["rr"], writes=["cs_d"], key="cso")
    P.barrier()
    A.reset(PH)

    def rstd_block(srcs, dim, sq, ss_bank, ss_tok, rs, rstd, rtok, src_reads, tg=""):
        nsrc = len(srcs)
        nb = len(sq)
        for i, src in enumerate(srcs):
            P.act(lambda e, src=src, i=i: e.activation(sq[i % nb], src, AF.Square),
                  reads=src_reads[i], writes=[f"sq{i % nb}{tg}"])
            P.pe(lambda e, i=i: e.matmul(ss_bank[:], onesb, sq[i % nb], start=(i == 0), stop=(i == nsrc - 1)),
                 reads=[f"sq{i % nb}{tg}", "onesb"], writes=[ss_tok])
        P.act(lambda e: e.activation(rs, ss_bank[:], AF.Ln, bias=eps_col, scale=1.0 / dim),
              reads=[ss_tok, "vecs"], writes=["rs" + tg])
        P.act(lambda e: e.activation(rstd, rs, AF.Exp, scale=-0.5), reads=["rs" + tg], writes=[rtok])

    def dump_x(normed):
        m = A.mark()
        sq = [A.alloc([T], BF16) for _ in range(8)]
        rs = A.alloc([T], F32)
        rstd = A.alloc([T], F32)
        o32 = A.alloc([NCH, T], F32)
        stage = A.alloc([4, D], F32)
        for tb in range(NTB):
            t0, t1 = tb * T, (tb + 1) * T
            if normed:
                rstd_block([xT[:, c, t0:t1] for c in range(NCH)], D, sq, ps[7], "ps7", rs, rstd, "rstd",
                           [xtok(c, t0, t1) for c in range(NCH)])
                for c in range(NCH):
                    P.dve(lambda e, c=c, t0=t0, t1=t1: e.scalar_tensor_tensor(
                        o32[:, c, :], xT[:, c, t0:t1], vc("fg", c), rstd, ALU.mult, ALU.mult),
                        reads=xtok(c, t0, t1) + ["rstd", "vecs"], writes=[f"o32_{c}"])
            for i in range(4):
                for hh in range(2):
                    bank = ps[(i * 2 + hh) % 4]
                    btok = f"ps{(i * 2 + hh) % 4}"
                    for cc in range(4):
                        c = hh * 4 + cc
                        src = o32[:, c, i * 128:(i + 1) * 128] if normed else xT[:, c, t0 + i * 128:t0 + (i + 1) * 128]
                        rd = [f"o32_{c}"] if normed else xtok(c, t0, t1)
                        P.pe(lambda e, bank=bank, src=src, cc=cc: e.transpose(
                            bank[:, cc * 128:(cc + 1) * 128], src, identf), reads=rd + ["cst"], writes=[btok])
                    dst = stage[:, i, hh * 512:(hh + 1) * 512]
                    if hh == 0:
                        P.dve(lambda e, dst=dst, bank=bank: e.tensor_copy(dst, bank[:]), reads=[btok],
                              writes=[f"stage{i}"])
                    else:
                        P.act(lambda e, dst=dst, bank=bank: e.activation(dst, bank[:], AF.Copy), reads=[btok],
                              writes=[f"stage{i}"])
            P.dma(SP, lambda e, t0=t0, t1=t1: e.dma_start(
                out=y_d[t0:t1, :].rearrange("(i p) d -> p i d", p=128), in_=stage),
                reads=[f"stage{i}" for i in range(4)], writes=["y"], key="yout")
        P.barrier()
        A.reset(m)

    def finish():
        P.emit()
        return nc

    def ffn_phase(l):
        m = A.mark()
        NB = 3
        sq = [A.alloc([T], BF16) for _ in range(2)]
        hT = A.alloc([NCH, T], BF16)
        u = A.alloc([NF, T], BF16)
        NW = 4
        wup = [A.alloc([8, 2, 128], BF16) for _ in range(NW)]
        wd = [A.alloc([NF, 128], BF16) for _ in range(2)]
        a_sb = [A.alloc([T + 2], F32) for _ in range(NB)]
        t1b = [A.alloc([T], F32) for _ in range(NB)]
        halo = A.alloc([NF, 2], F32)
        tmp, tmptok = t1b[0], "t10"
        rs, rstok = t1b[1], "t11"
        rstd = rs
        P.dve(lambda e: e.memset(halo, 0.0), writes=[f"halo{j}" for j in range(NF)])
        wup_src = w_up[l].rearrange("(kc p) (h j m) -> p kc h j m", p=128, h=2, m=128)
        wd_src = w_down[l].rearrange("(j p) (c m) -> p j c m", p=128, m=128)

        up_list = [(tb, j) for tb in range(NTB) for j in range(NF)]
        dn_list = [(tb, c) for tb in range(NTB) for c in range(NCH)]
        st = {"up": 0, "dn": 0}

        def issue_up(upto):
            while st["up"] < min(upto, len(up_list)):
                n = st["up"]
                _, j = up_list[n]
                wb = wup[n % NW]
                wtok = f"wup{n % NW}"
                P.dma(POOL, lambda e, wb=wb, j=j: e.dma_start(out=wb[:, :, 0, :], in_=wup_src[:, :, 0, j, :]),
                      writes=[wtok + "a"], key=wtok + "a")
                P.dma(POOL, lambda e, wb=wb, j=j: e.dma_start(out=wb[:, :, 1, :], in_=wup_src[:, :, 1, j, :]),
                      writes=[wtok + "v"], key=wtok + "v")
                st["up"] += 1

        def issue_dn(upto):
            while st["dn"] < min(upto, len(dn_list)):
                n = st["dn"]
                _, c = dn_list[n]
                wdb = wd[n % 2]
                P.dma(POOL, lambda e, wdb=wdb, c=c: e.dma_start(out=wdb, in_=wd_src[:, :, c, :]),
                      writes=[f"wd{n % 2}"], key=f"wd{n % 2}")
                st["dn"] += 1

        def stats_h(tb):
            t0, t1 = tb * T, (tb + 1) * T
            srcs = [xT[:, c, t0:t1] for c in range(NCH)]
            for i, src in enumerate(srcs):
                P.act(lambda e, src=src, i=i: e.activation(sq[i % 2], src, AF.Square),
                      reads=xtok(i, t0, t1), writes=[f"sq{i % 2}"])
                P.pe(lambda e, i=i: e.matmul(ps[0][:], onesb, sq[i % 2], start=(i == 0), stop=(i == NCH - 1)),
                     reads=[f"sq{i % 2}", "onesb"], writes=["ps0"])
            P.act(lambda e: e.activation(rs, ps[0][:], AF.Ln, bias=eps_col, scale=1.0 / D),
                  reads=["ps0", "vecs"], writes=[rstok])
            P.act(lambda e: e.activation(rstd, rs, AF.Exp, scale=-0.5), reads=[rstok], writes=[rstok])
            for c in range(NCH):
                tm, tmt = (t1b[0], "t10") if c % 2 == 0 else (t1b[2], "t12")
                P.dve(lambda e, c=c, t0=t0, t1=t1, tm=tm: e.tensor_tensor(tm, xT[:, c, t0:t1], rstd, ALU.mult),
                      reads=xtok(c, t0, t1) + [rstok], writes=[tmt])
                P.act(lambda e, c=c, tm=tm: e.activation(hT[:, c, :], tm, AF.Identity,
                                                         bias=mcol(l, 3, c), scale=acol(l, 1, c)),
                      reads=[tmt, "mods", "coef"], writes=[f"hT{c}"])

        def down(tb, c):
            t0, t1 = tb * T, (tb + 1) * T
            n = tb * NCH + c
            issue_dn(n + 2)
            wdb = wd[n % 2]
            wdtok = f"wd{n % 2}"
            yb = ps[6 + c % 2]
            ytok = f"ps{6 + c % 2}"
            for j in range(NF):
                P.pe(lambda e, yb=yb, wdb=wdb, j=j: e.matmul(yb[:], wdb[:, j, :], u[:, j, :],
                                                              start=(j == 0), stop=(j == NF - 1)),
                     reads=[wdtok, f"u{j}"], writes=[ytok])
            P.dve(lambda e, yb=yb, c=c, t0=t0, t1=t1: e.scalar_tensor_tensor(
                xT[:, c, t0:t1], yb[:], mcol(l, 5, c), xT[:, c, t0:t1], ALU.mult, ALU.add),
                reads=[ytok, "mods"] + xtok(c, t0, t1), writes=xtok(c, t0, t1))

        issue_up(NW - 1)
        stats_h(0)
        pend2 = [None]

        def stage2_flush():
            if pend2[0] is not None:
                pend2[0]()
                pend2[0] = None

        for tb in range(NTB):
            for j in range(NF):
                n = tb * NF + j
                issue_up(n + NW)
                if j == NF - 6:
                    issue_dn(tb * NCH + 2)
                wb = wup[n % NW]
                wtok = f"wup{n % NW}"
                ab, vb = ps[n % NB], ps[3 + n % NB]
                atok, vtok = f"ps{n % NB}", f"ps{3 + n % NB}"
                for kc in range(8):
                    P.pe(lambda e, ab=ab, wb=wb, kc=kc: e.matmul(ab[:], wb[:, kc, 0, :], hT[:, kc, :],
                                                                  start=(kc == 0), stop=(kc == 7)),
                         reads=[wtok + "a", f"hT{kc}"], writes=[atok])
                for kc in range(8):
                    P.pe(lambda e, vb=vb, wb=wb, kc=kc: e.matmul(vb[:], wb[:, kc, 1, :], hT[:, kc, :],
                                                                  start=(kc == 0), stop=(kc == 7)),
                         reads=[wtok + "v", f"hT{kc}"], writes=[vtok])
                asb = a_sb[n % NB]
                t1_ = t1b[n % NB]
                astok, ttok = f"asb{n % NB}", f"t1{n % NB}"
                cw = lambda k, j=j: vc("convw", (l * 3 + k) * NF + j)
                cb = vc("convb", l * NF + j)
                P.dve(lambda e, asb=asb, j=j: e.tensor_copy(asb[:, 0:2], halo[:, j, :]),
                      reads=[f"halo{j}"], writes=[astok + "h"])
                P.act(lambda e, asb=asb, ab=ab: e.activation(asb[:, 2:T + 2], ab[:], AF.Copy),
                      reads=[atok], writes=[astok])
                P.act(lambda e, t1_=t1_, ab=ab, cw=cw, cb=cb: e.activation(t1_, ab[:], AF.Identity,
                                                                          bias=cb, scale=cw(2)),
                      reads=[atok, "vecs"], writes=[ttok])
                stage2_flush()
                P.dve(lambda e, t1_=t1_, asb=asb, cw=cw: e.scalar_tensor_tensor(
                    t1_, asb[:, 1:T + 1], cw(1), t1_, ALU.mult, ALU.add),
                    reads=[astok, astok + "h", ttok, "vecs"], writes=[ttok])
                P.dve(lambda e, t1_=t1_, asb=asb, cw=cw: e.scalar_tensor_tensor(
                    t1_, asb[:, 0:T], cw(0), t1_, ALU.mult, ALU.add),
                    reads=[astok, astok + "h", ttok, "vecs"], writes=[ttok])
                P.dve(lambda e, asb=asb, j=j: e.tensor_copy(halo[:, j, :], asb[:, T:T + 2]),
                      reads=[astok], writes=[f"halo{j}"])

                def stage2(t1_=t1_, vb=vb, j=j, ttok=ttok, vtok=vtok):
                    P.act(lambda e: e.activation(t1_, t1_, AF.Gelu), reads=[ttok], writes=[ttok])
                    P.dve(lambda e: e.tensor_tensor(u[:, j, :], t1_, vb[:], ALU.mult),
                          reads=[ttok, vtok], writes=[f"u{j}"])
                pend2[0] = stage2
            stage2_flush()
            down(tb, 0)
            if tb + 1 < NTB:
                stats_h(tb + 1)
            for c in range(1, NCH):
                down(tb, c)
        P.barrier()
        A.reset(m)

    def pool_phase(l):
        m = A.mark()
        sq = [A.alloc([T], BF16) for _ in range(4)]
        rs = A.alloc([T], F32)
        rstd_all = A.alloc([S], F32)
        HW = S // 2
        hp = A.alloc([16 + HW], F32)
        sA = A.alloc([16 + HW], F32)
        sB = A.alloc([16 + HW], F32)
        pooled = A.alloc([2, S], BF16)
        pw = A.alloc([4, 2, 256], BF16)
        ytmp = [A.alloc([T], F32) for _ in range(2)]
        fix = A.alloc([16], F32)
        pwf = sB[:, 0:2048].rearrange("p (g k n) -> p g k n", g=4, k=2)
        P.dma(SP, lambda e: e.dma_start(out=pwf, in_=pool_w[l].rearrange("g (kc p) n -> p g kc n", p=128)),
              writes=["sB"], key="pw")
        for g_ in range(4):
            for kc_ in range(2):
                P.dve(lambda e, g_=g_, kc_=kc_: e.tensor_scalar(
                    pw[:, g_, kc_, :], pwf[:, g_, kc_, :], acol(l, 0, 2 * g_ + kc_), None, ALU.mult),
                    reads=["sB", "coef"], writes=["pw"])
        for tb in range(NTB):
            t0, t1 = tb * T, (tb + 1) * T
            rstd_block([xT[:, c, t0:t1] for c in range(NCH)], D, sq, ps[6], "ps6", rs, rstd_all[:, t0:t1],
                       f"rstd{tb}", [xtok(c, t0, t1) for c in range(NCH)])
        P.dve(lambda e: e.memset(hp[:, 0:16], 0.0), writes=["hp"])
        P.dve(lambda e: e.memset(sA[:, 0:16], 0.0), writes=["sA"])
        P.dve(lambda e: e.memset(sB[:, 0:16], 0.0), writes=["sB"])
        rall = [f"rstd{tb}" for tb in range(NTB)]
        ny = 0
        for g in range(4):
            w = WINDOWS[g]
            for mo in range(2):
                c = 2 * g + mo
                for hf in range(2):
                    base = hf * HW
                    lo = 16 if hf == 0 else 0
                    tlo = base - 16 + lo
                    ncol = 16 + HW - lo
                    xt = xtok(c, max(tlo, 0), base + HW)
                    if hf == 0:
                        P.dve(lambda e: e.memset(hp[:, 0:16], 0.0), reads=["hp"], writes=["hp"])
                    P.dve(lambda e, lo=lo, tlo=tlo, ncol=ncol, c=c: e.tensor_tensor(
                        hp[:, lo:lo + ncol], xT[:, c, tlo:tlo + ncol], rstd_all[:, tlo:tlo + ncol], ALU.mult),
                        reads=xt + rall, writes=["hp"])
                    src, stok = hp, "hp"
                    bufs = [(sA, "sA"), (sB, "sB")]
                    sh = 1
                    k = 0
                    W_ = 16 + HW
                    while sh < w:
                        dstb, dtok = bufs[k % 2]
                        first = 2 * sh - 1
                        P.dve(lambda e, dstb=dstb, src=src, sh=sh, first=first: e.tensor_tensor(
                            dstb[:, first:W_], src[:, first:W_], src[:, first - sh:W_ - sh], ALU.add),
                            reads=[stok], writes=[dtok])
                        src, stok = dstb, dtok
                        sh *= 2
                        k += 1
                    P.dve(lambda e, src=src, mo=mo, base=base, w=w: e.scalar_tensor_tensor(
                        pooled[:, mo, base:base + HW], src[:, 16:16 + HW], 1.0 / w, hp[:, 16:16 + HW],
                        ALU.mult, ALU.subtract),
                        reads=[stok, "hp"], writes=[f"pooled{mo}_{hf}"])
                    if hf == 0:
                        P.dve(lambda e, src=src, g=g: e.tensor_tensor(
                            fix, src[:, 16:32], vc("invc", g * 16, 16), ALU.mult),
                            reads=[stok, "vecs"], writes=["fix"])
                        P.dve(lambda e, mo=mo: e.tensor_tensor(
                            pooled[:, mo, 0:16], fix, hp[:, 16:32], ALU.subtract),
                            reads=["fix", "hp", f"pooled{mo}_0"], writes=[f"pooled{mo}_0"])
            for tb in range(NTB):
                t0, t1 = tb * T, (tb + 1) * T
                hfb = tb // (NTB // 2)
                for mo in range(2):
                    c = 2 * g + mo
                    yb = ps[ny % 2]
                    ytok = f"ps{ny % 2}"
                    yt = ytmp[ny % 2]
                    yttok = f"ytmp{ny % 2}"
                    ny += 1
                    for kc in range(2):
                        P.pe(lambda e, yb=yb, kc=kc, mo=mo, g=g, t0=t0, t1=t1: e.matmul(
                            yb[:], pw[:, g, kc, mo * 128:(mo + 1) * 128], pooled[:, kc, t0:t1],
                            start=(kc == 0), stop=(kc == 1)),
                            reads=["pw", f"pooled{kc}_{hfb}"], writes=[ytok])
                    P.act(lambda e, yb=yb, yt=yt, c=c: e.activation(
                        yt, yb[:], AF.Identity, bias=pcoef[:, (l * 2 + 1) * 8 + c:(l * 2 + 1) * 8 + c + 1],
                        scale=pcoef[:, (l * 2) * 8 + c:(l * 2) * 8 + c + 1]),
                        reads=[ytok, "pcoef"], writes=[yttok])
                    P.pool(lambda e, yt=yt, c=c, t0=t0, t1=t1: e.tensor_tensor(
                        xT[:, c, t0:t1], xT[:, c, t0:t1], yt, ALU.add),
                        reads=[yttok] + xtok(c, t0, t1), writes=xtok(c, t0, t1))
        P.barrier()
        A.reset(m)

    def kv_phase():
        m = A.mark()
        sq = [A.alloc([T], BF16) for _ in range(8)]
        rs = A.alloc([T], F32)
        rstd = A.alloc([T], F32)
        hT = A.alloc([NCH, T], BF16)
        wkv = A.alloc([8, KVR + 2 * ROPE], BF16)
        ckvb = A.alloc([2, T], BF16)
        csb = A.alloc([2, T], F32, parts=64)
        t1_ = A.alloc([T], F32, parts=64)
        t2_ = A.alloc([T], F32, parts=64)
        krb = A.alloc([T], BF16, parts=64)
        src = w_dkv.rearrange("(kc p) n -> p kc n", p=128)
        P.dma(POOL, lambda e: e.dma_start(out=wkv[:, :, 0:320], in_=src), writes=["wkv"], key="wkv")
        P.dma(POOL, lambda e: e.dma_start(out=wkv[:, :, 320:352], in_=src[:, :, 288:320]), writes=["wkv"], key="wkv")
        P.dma(POOL, lambda e: e.dma_start(out=wkv[:, :, 352:384], in_=src[:, :, 256:288]), writes=["wkv"], key="wkv")
        mpieces = mods_pieces(2, ps[5], "ps5")
        for tb in range(NTB):
            t0, t1 = tb * T, (tb + 1) * T
            rstd_block([xT[:, c, t0:t1] for c in range(NCH)], D, sq, ps[6], "ps6", rs, rstd, "rstd",
                       [xtok(c, t0, t1) for c in range(NCH)])
            for c in range(NCH):
                P.dve(lambda e, c=c, t0=t0, t1=t1: e.scalar_tensor_tensor(
                    hT[:, c, :], xT[:, c, t0:t1], vc("kvg", c), rstd, ALU.mult, ALU.mult),
                    reads=xtok(c, t0, t1) + ["rstd", "vecs"], writes=[f"hT{c}"])
            P.dma(SP, lambda e, t0=t0, t1=t1: e.dma_start(out=csb, in_=cs_d[:, :, t0:t1].rearrange("w p t -> p w t")),
                  writes=["csb"], key="csb")
            outs = [(ps[0], "ps0", 0, 128, 128), (ps[1], "ps1", 128, 256, 128),
                    (ps[2], "ps2", 256, 320, 64), (ps[3], "ps3", 320, 384, 64)]
            for bank, btok, c0, c1, mm in outs:
                for kc in range(8):
                    P.pe(lambda e, bank=bank, c0=c0, c1=c1, mm=mm, kc=kc: e.matmul(
                        bank[0:mm, :], wkv[:, kc, c0:c1], hT[:, kc, :], start=(kc == 0), stop=(kc == 7)),
                        reads=["wkv", f"hT{kc}"], writes=[btok])
            rstd_block([ps[0][:], ps[1][:]], KVR, sq, ps[7], "ps7", rs, rstd, "rstd", [["ps0"], ["ps1"]])
            for mi in range(2):
                P.dve(lambda e, mi=mi: e.scalar_tensor_tensor(
                    ckvb[:, mi, :], ps[mi][:], vc("ckvg", mi), rstd, ALU.mult, ALU.mult),
                    reads=[f"ps{mi}", "rstd", "vecs"], writes=["ckvb"])
            P.dma(SP, lambda e, t0=t0, t1=t1: e.dma_start(out=ckv_d[:, :, t0:t1], in_=ckvb),
                  reads=["ckvb"], writes=["ckv_d"], key="ckvo")
            P.dve(lambda e: e.tensor_tensor(t1_, ps[2][0:64, :], csb[:, 0, :], ALU.mult),
                  reads=["ps2", "csb"], writes=["t1"])
            P.dve(lambda e: e.tensor_tensor(t2_, ps[3][0:64, :], csb[:, 1, :], ALU.mult),
                  reads=["ps3", "csb"], writes=["t2"])
            P.dve(lambda e: e.tensor_tensor(krb, t1_, t2_, ALU.add), reads=["t1", "t2"], writes=["krb"])
            P.dma(SP, lambda e, t0=t0, t1=t1: e.dma_start(out=kr_d[:, t0:t1], in_=krb),
                  reads=["krb"], writes=["kr_d"], key="kro")
            for _ in range(2):
                if mpieces:
                    mpieces.pop(0)()
        while mpieces:
            mpieces.pop(0)()
        P.barrier()
        A.reset(m)

    def mla_q_phase(l):
        jl = l - 2
        m = A.mark()
        B2 = []
        sq8 = [A.alloc([T], BF16) for _ in range(8)]
        tmp8 = [A.alloc([T], F32) for _ in range(8)]
        for par in range(2):
            B2.append(dict(sq=sq8, rs=A.alloc([T], F32), rstd=A.alloc([T], F32),
                           tmp=tmp8, hT=A.alloc([NCH, T], BF16), cqb=A.alloc([3, T], BF16)))
        wdq = A.alloc([8, QR], BF16)
        P.dma(POOL, lambda e: e.dma_start(out=wdq, in_=w_dq[jl].rearrange("(kc p) n -> p kc n", p=128)),
              writes=["wdq"], key="wdq")
        mpieces = mods_pieces(3, ps[7], "ps7", None, 256) if l == 2 else []

        def ctx(tb):
            par = tb % 2
            b = B2[par]
            return par, f"_{par}", b, 3 * par, tb * T, (tb + 1) * T

        def S1H(tb):
            par, tg, b, pb, t0, t1 = ctx(tb)
            rstd_block([xT[:, c, t0:t1] for c in range(NCH)], D, b["sq"], ps[6], "ps6", b["rs"], b["rstd"],
                       "rstd" + tg, [xtok(c, t0, t1) for c in range(NCH)], "")
            for c in range(NCH):
                P.dve(lambda e, c=c, t0=t0, t1=t1, tmp=b["tmp"], rstd=b["rstd"]: e.tensor_tensor(
                    tmp[c], xT[:, c, t0:t1], rstd, ALU.mult),
                    reads=xtok(c, t0, t1) + ["rstd" + tg], writes=[f"tmp{c}"])
                P.act(lambda e, c=c, tmp=b["tmp"], hT=b["hT"]: e.activation(
                    hT[:, c, :], tmp[c], AF.Identity, bias=mcol(l, 0, c), scale=acol(l, 0, c)),
                    reads=[f"tmp{c}", "mods", "coef"], writes=[f"hT{c}{tg}"])

        def M(tb):
            par, tg, b, pb, t0, t1 = ctx(tb)
            for mi in range(3):
                for kc in range(8):
                    P.pe(lambda e, mi=mi, kc=kc, hT=b["hT"], pb=pb: e.matmul(
                        ps[pb + mi][:], wdq[:, kc, mi * 128:(mi + 1) * 128], hT[:, kc, :],
                        start=(kc == 0), stop=(kc == 7)),
                        reads=["wdq", f"hT{kc}{tg}"], writes=[f"ps{pb + mi}"])

        def S2C(tb):
            par, tg, b, pb, t0, t1 = ctx(tb)
            rstd_block([ps[pb][:], ps[pb + 1][:], ps[pb + 2][:]], QR, b["sq"], ps[6], "ps6", b["rs"], b["rstd"],
                       "rstd" + tg, [[f"ps{pb}"], [f"ps{pb + 1}"], [f"ps{pb + 2}"]], "")
            for mi in range(3):
                P.dve(lambda e, mi=mi, cqb=b["cqb"], rstd=b["rstd"], pb=pb: e.scalar_tensor_tensor(
                    cqb[:, mi, :], ps[pb + mi][:], vc("qng", jl * 3 + mi), rstd, ALU.mult, ALU.mult),
                    reads=[f"ps{pb + mi}", "rstd" + tg, "vecs"], writes=["cqb" + tg])
            P.dma(SP, lambda e, t0=t0, t1=t1, cqb=b["cqb"]: e.dma_start(out=cq_d[:, :, t0:t1], in_=cqb),
                  reads=["cqb" + tg], writes=[f"cq_d{tb}"], key="cqo" + tg)

        S1H(0)
        M(0)
        for tb in range(NTB):
            for _ in range(3):
                if mpieces:
                    mpieces.pop(0)()
            if tb + 1 < NTB:
                S1H(tb + 1)
            S2C(tb)
            if tb + 1 < NTB:
                M(tb + 1)
        while mpieces:
            mpieces.pop(0)()
        P.barrier()
        A.reset(m)

    def mla_attn_phase(l):
        jl = l - 2
        m = A.mark()
        qn = A.alloc([S], BF16)
        qra = A.alloc([S], BF16)
        kn = A.alloc([S], BF16)
        kra = A.alloc([S], BF16)
        vsb = A.alloc([S // 128, 128], BF16)
        og = [A.alloc([T], BF16) for _ in range(2)]
        cqbs = [A.alloc([3, T], BF16) for _ in range(2)]
        ckvbs = [A.alloc([2, T], BF16) for _ in range(2)]
        csb = A.alloc([2, T], F32)
        R1 = A.alloc([T], F32)
        R2 = A.alloc([3 * T], BF16)
        t1_ = R1[0:64, :]
        rl = R1
        t2_ = R2[:, 0:2 * T].bitcast(F32)[0:64, :]
        PT = [R2[:, 0:T], R2[:, T:2 * T], R2[:, 2 * T:3 * T]]
        wq = A.alloc([3, 256], BF16)
        wk = A.alloc([2, 128], BF16)
        wv = A.alloc([2, 128], BF16)
        maskg = A.alloc([4, T], BF16)
        mx = [A.alloc([8], F32) for _ in range(2)]
        mrow = [A.alloc([2], F32) for _ in range(2)]
        negm65 = [A.alloc([66], BF16) for _ in range(2)]
        P.dma(SP, lambda e: e.dma_start(out=kra[0:64, :], in_=kr_d), writes=["kr"], key="krl")
        P.dve(lambda e: e.memset(kra[64:65, :], 1.0), writes=["kr1"])
        for par in range(2):
            P.dve(lambda e, par=par: e.memset(negm65[par], 0.0), writes=[f"negm{par}"])
        P.dma(POOL, lambda e: e.dma_start(out=maskg, in_=dr["maskg"].rearrange("p (r q) -> p r q", r=4)),
              writes=["maskg"], key="maskg")
        wq_src = w_uq[jl].rearrange("(kc p) (h n) -> p kc h n", p=128, n=192)
        wk_src = w_uk.rearrange("(kc p) (h n) -> p kc h n", p=128, n=128)
        wv_src = w_uv.rearrange("(kc p) (h n) -> p kc h n", p=128, n=128)
        NQ = S // 128
        for h in range(NH):
            P.dma(POOL, lambda e, h=h: e.dma_start(out=wq[:, :, 0:192], in_=wq_src[:, :, h, :]),
                  writes=["wq"], key="wq")
            P.dma(POOL, lambda e, h=h: e.dma_start(out=wq[:, :, 192:224], in_=wq_src[:, :, h, 160:192]),
                  writes=["wq"], key="wq")
            P.dma(POOL, lambda e, h=h: e.dma_start(out=wq[:, :, 224:256], in_=wq_src[:, :, h, 128:160]),
                  writes=["wq"], key="wq")
            P.dma(POOL, lambda e, h=h: e.dma_start(out=wk, in_=wk_src[:, :, h, :]), writes=["wk"], key="wk")
            P.dma(POOL, lambda e, h=h: e.dma_start(out=wv, in_=wv_src[:, :, h, :]), writes=["wv"], key="wv")
            for tb in range(NTB):
                t0, t1 = tb * T, (tb + 1) * T
                cqb, ckvb = cqbs[tb % 2], ckvbs[tb % 2]
                cqtok, ckvtok = f"cqb{tb % 2}", f"ckvb{tb % 2}"
                P.dma(SP, lambda e, t0=t0, t1=t1, cqb=cqb: e.dma_start(out=cqb, in_=cq_d[:, :, t0:t1]),
                      writes=[cqtok], key=cqtok)
                P.dma(SP, lambda e, t0=t0, t1=t1, ckvb=ckvb: e.dma_start(out=ckvb, in_=ckv_d[:, :, t0:t1]),
                      writes=[ckvtok], key=ckvtok)
                P.dma(SP, lambda e, t0=t0, t1=t1: e.dma_start(
                    out=csb[0:64], in_=cs_d[:, :, t0:t1].rearrange("w p t -> p w t")), writes=["csb"], key="csb")
                for (bank, btok, c0, c1, mm) in [(ps[0], "ps0", 0, 128, 128), (ps[1], "ps1", 128, 192, 64),
                                                  (ps[2], "ps2", 192, 256, 64)]:
                    for kc in range(3):
                        P.pe(lambda e, bank=bank, c0=c0, c1=c1, mm=mm, kc=kc, cqb=cqb: e.matmul(
                            bank[0:mm, :], wq[:, kc, c0:c1], cqb[:, kc, :], start=(kc == 0), stop=(kc == 2)),
                            reads=["wq", cqtok], writes=[btok])
                for kc in range(2):
                    P.pe(lambda e, kc=kc, ckvb=ckvb: e.matmul(ps[3][:], wk[:, kc, :], ckvb[:, kc, :],
                                                   start=(kc == 0), stop=(kc == 1)),
                         reads=["wk", ckvtok], writes=["ps3"])
                P.act(lambda e, t0=t0, t1=t1: e.activation(qn[:, t0:t1], ps[0][:], AF.Identity, scale=QSCALE),
                      reads=["ps0"], writes=[f"qn{tb}"])
                P.act(lambda e, t0=t0, t1=t1: e.activation(kn[:, t0:t1], ps[3][:], AF.Copy),
                      reads=["ps3"], writes=[f"kn{tb}"])
                P.dve(lambda e: e.scalar_tensor_tensor(t1_, ps[1][0:64, :], QSCALE, csb[0:64, 0, :], ALU.mult, ALU.mult),
                      reads=["ps1", "csb"], writes=["R1"])
                P.dve(lambda e: e.scalar_tensor_tensor(t2_, ps[2][0:64, :], QSCALE, csb[0:64, 1, :], ALU.mult, ALU.mult),
                      reads=["ps2", "csb"], writes=["PT0", "PT1"])
                P.dve(lambda e, t0=t0, t1=t1: e.tensor_tensor(qra[0:64, t0:t1], t1_, t2_, ALU.add),
                      reads=["R1", "PT0", "PT1"], writes=[f"qr{tb}"])
                for i in range(4):
                    ti = tb * 4 + i
                    vb = ps[4 + i % 2]
                    vtok = f"ps{4 + i % 2}"
                    for kc in range(2):
                        P.pe(lambda e, vb=vb, kc=kc, i=i, ckvb=ckvb: e.matmul(
                            vb[:, 0:128], ckvb[:, kc, i * 128:(i + 1) * 128], wv[:, kc, :],
                            start=(kc == 0), stop=(kc == 1)), reads=["wv", ckvtok], writes=[vtok])
                    P.act(lambda e, vb=vb, ti=ti: e.activation(vsb[:, ti, :], vb[:, 0:128], AF.Copy),
                          reads=[vtok], writes=[f"v{ti}"])
            allk = [f"kn{tb}" for tb in range(NTB)] + ["kr", "kr1"]
            P.dve(lambda e: e.memset(qra[64:65, :], 0.0), writes=[f"qm{i}" for i in range(NQ)])
            p1n = [0]

            def pass1_chunks(i):
                par = i % 2
                G = i // 4
                nk = (i + 1) * 128
                nchunk = (nk + 511) // 512
                qtok = [f"qn{G}", f"qr{G}", f"qm{i}"]
                out = []
                for cidx in range(nchunk):
                    def chunk(cidx=cidx):
                        k0 = cidx * 512
                        k1 = min(nk, k0 + 512)
                        bank = ps[p1n[0] % 2]
                        btok = f"ps{p1n[0] % 2}"
                        p1n[0] += 1
                        isdiag = (cidx == nchunk - 1)
                        P.pe(lambda e: e.matmul(
                            bank[:, 0:k1 - k0], qn[:, i * 128:(i + 1) * 128], kn[:, k0:k1], start=True, stop=False),
                            reads=qtok + allk, writes=[btok])
                        P.pe(lambda e: e.matmul(
                            bank[:, 0:k1 - k0], qra[0:65, i * 128:(i + 1) * 128], kra[0:65, k0:k1], start=False,
                            stop=(not isdiag)), reads=qtok + allk, writes=[btok])
                        if isdiag:
                            P.pe(lambda e: e.matmul(
                                bank[:, k1 - k0 - 128:k1 - k0], identb, maskqb, start=False, stop=True),
                                reads=["identb", "maskqb"], writes=[btok])
                        P.dve(lambda e: e.reduce_max(mx[par][:, cidx:cidx + 1], bank[:, 0:k1 - k0], AX.X),
                              reads=[btok], writes=[f"mx{par}"])
                        if isdiag:
                            P.dve(lambda e: e.reduce_max(mrow[par][:, 0:1], mx[par][:, 0:nchunk], AX.X),
                                  reads=[f"mx{par}"], writes=[f"mrow{par}"])
                            P.dve(lambda e: e.tensor_scalar(negm65[par][:, 64:65], mrow[par][:, 0:1], -1.0, None,
                                                            ALU.mult),
                                  reads=[f"mrow{par}"], writes=[f"negm{par}"])
                    out.append(chunk)

                def tail():
                    P.pe(lambda e: e.matmul(ps[7][0:65, par * 128:(par + 1) * 128], negm65[par][:, 0:65],
                                            identb, start=True, stop=True),
                         reads=[f"negm{par}", "identb"], writes=["ps7"])
                    P.act(lambda e: e.activation(qra[64:65, i * 128:(i + 1) * 128],
                                                 ps[7][64:65, par * 128:(par + 1) * 128], AF.Copy),
                          reads=["ps7"], writes=[f"qm{i}"])
                return out, tail

            stn = [0]

            def pass2(G, fillers):
                g0, g1 = G * T, (G + 1) * T
                nkb = 4 * G + 4
                OT, ottok = ps[4], "ps4"
                Lb = ps[5]
                qtok = [f"qn{G}", f"qr{G}"] + [f"qm{4 * G + q}" for q in range(4)]
                pend = []
                for j in range(nkb):
                    sidx = stn[0] % 3
                    stn[0] += 1
                    bi = [2, 3, 6][sidx]
                    bank = ps[bi]
                    btok = f"ps{bi}"
                    pt = PT[sidx]
                    pttok = f"PT{sidx}"
                    P.pe(lambda e, bank=bank, j=j: e.matmul(
                        bank[:], kn[:, j * 128:(j + 1) * 128], qn[:, g0:g1], start=True, stop=False),
                        reads=qtok + allk, writes=[btok])
                    P.pe(lambda e, bank=bank, j=j: e.matmul(
                        bank[:], kra[0:65, j * 128:(j + 1) * 128], qra[0:65, g0:g1], start=False,
                        stop=(j < 4 * G)), reads=qtok + allk, writes=[btok])
                    if j >= 4 * G:
                        P.pe(lambda e, bank=bank, j=j: e.matmul(
                            bank[:], identb, maskg[:, j - 4 * G, :], start=False, stop=True),
                            reads=["identb", "maskg"], writes=[btok])
                    P.act(lambda e, pt=pt, bank=bank: e.activation(pt, bank[:], AF.Exp),
                          reads=[btok], writes=[pttok])

                    def pv(j=j, pt=pt, pttok=pttok):
                        P.pe(lambda e: e.matmul(OT[:], vsb[:, j, :], pt, start=(j == 0), stop=(j == nkb - 1)),
                             reads=[pttok, f"v{j}"], writes=[ottok])
                        P.pe(lambda e: e.matmul(Lb[:], onesb, pt, start=(j == 0), stop=(j == nkb - 1)),
                             reads=[pttok, "onesb"], writes=["ps5"])
                    pend.append(pv)
                    if len(pend) > 2:
                        pend.pop(0)()
                    if fillers:
                        fillers.pop(0)()
                for pv_ in pend:
                    pv_()
                while fillers:
                    fillers.pop(0)()
                P.act(lambda e: e.activation(rl, Lb[:], AF.Ln), reads=["ps5"], writes=["R1"])
                P.act(lambda e: e.activation(rl, rl, AF.Exp, scale=-1.0), reads=["R1"], writes=["R1"])
                ogb, ogtok = og[G % 2], f"og{G % 2}"
                P.dve(lambda e: e.tensor_tensor(ogb, OT[:], rl, ALU.mult),
                      reads=[ottok, "R1"], writes=[ogtok])
                P.dma(SP, lambda e, h=h: e.dma_start(out=oT_d[:, h, g0:g1], in_=ogb),
                      reads=[ogtok], writes=[f"oT_d{G}"], key=ogtok)

            def group_fillers(G):
                fl = []
                tails = []
                for i in range(4 * G, 4 * G + 4):
                    chunks, tail = pass1_chunks(i)
                    for ch in chunks:
                        fl.append(ch)
                        tails = [(tl, n - 1) for tl, n in tails]
                        while tails and tails[0][1] <= 0:
                            fl.append(tails.pop(0)[0])
                    tails.append((tail, 2))
                fl += [tl for tl, _ in tails]
                return fl

            for f_ in group_fillers(0):
                f_()
            for G in range(NTB):
                fillers = group_fillers(G + 1) if G + 1 < NTB else []
                pass2(G, fillers)
        P.barrier()
        A.reset(m)

    def mla_o_phase(l):
        jl = l - 2
        m = A.mark()
        wo = A.alloc([NH, D], BF16)
        ob = [A.alloc([NH, T], BF16) for _ in range(2)]
        P.dma(POOL, lambda e: e.dma_start(out=wo, in_=w_o[jl].rearrange("(h p) n -> p h n", p=128)),
              writes=["wo"], key="wo")
        ny = 0
        for tb in range(NTB):
            t0, t1 = tb * T, (tb + 1) * T
            o_ = ob[tb % 2]
            otok = f"ob{tb % 2}"
            P.dma(SP, lambda e, o_=o_, t0=t0, t1=t1: e.dma_start(out=o_, in_=oT_d[:, :, t0:t1]),
                  writes=[otok], key=otok)
            for c in range(NCH):
                yb = ps[ny % 2]
                ytok = f"ps{ny % 2}"
                ny += 1
                for h in range(NH):
                    P.pe(lambda e, yb=yb, o_=o_, h=h, c=c: e.matmul(
                        yb[:], wo[:, h, c * 128:(c + 1) * 128], o_[:, h, :], start=(h == 0), stop=(h == NH - 1)),
                        reads=["wo", otok], writes=[ytok])
                P.dve(lambda e, yb=yb, c=c, t0=t0, t1=t1: e.scalar_tensor_tensor(
                    xT[:, c, t0:t1], yb[:], mcol(l, 2, c), xT[:, c, t0:t1], ALU.mult, ALU.add),
                    reads=[ytok, "mods"] + xtok(c, t0, t1), writes=xtok(c, t0, t1))
        P.barrier()
        A.reset(m)

    phases = []
    for l in range(DEPTH):
        if l < 2:
            phases.append((f"mix{l}", lambda l=l: pool_phase(l)))
        else:
            phases.append((f"q{l}", lambda l=l: mla_q_phase(l)))
            phases.append((f"attn{l}", lambda l=l: mla_attn_phase(l)))
            phases.append((f"mix{l}", lambda l=l: mla_o_phase(l)))
        phases.append((f"ffn{l}", lambda l=l: ffn_phase(l)))
        if l == 1:
            phases.append(("kv", kv_phase))
    if stop_after == "init":
        dump_x(False)
        return finish()
    for tag, fn in phases:
        fn()
        if stop_after == tag:
            dump_x(False)
            return finish()
    dump_x(True)
    return finish()


_PROG = {}


def _in_maps(inputs):
    cst = _consts()
    mg = _maskg()
    maps = []
    f32 = lambda a: np.ascontiguousarray(np.asarray(a, np.float32))
    shared = {k: f32(inputs[k]) for k in ["mod_w", "pool_w", "w_dkv", "w_uk", "w_uv", "w_dq", "w_uq", "w_o",
                                          "w_up", "w_down"]}
    x = np.asarray(inputs["x"], np.float32)
    pos = np.asarray(inputs["positions"], np.int32)
    for b in range(8):
        mp = dict(shared)
        mp["x"] = np.ascontiguousarray(x[b])
        mp["pos"] = np.ascontiguousarray(pos[b:b + 1])
        mp["vecs"] = _build_vecs(inputs, b)
        mp["cst"] = cst
        mp["maskg"] = mg
        maps.append(mp)
    return maps


def kernel(**inputs):
    if "full" not in _PROG:
        _PROG["full"] = build_program(None)
    nc = _PROG["full"]
    res = run_bass_kernel_spmd(nc, _in_maps(inputs), core_ids=list(range(8)))
    return np.stack([np.asarray(r["y"], np.float32) for r in res.results], axis=0)
```

```python
import contextlib
import numpy as np
import concourse.bass as bass
import concourse.mybir as mybir
from concourse.bass_utils import run_bass_kernel_spmd

F32 = mybir.dt.float32
BF16 = mybir.dt.bfloat16
I32 = mybir.dt.int32
AF = mybir.ActivationFunctionType
ALU = mybir.AluOpType
AX = mybir.AxisListType

PE, ACT, DVE, POOL, SP = "pe", "act", "dve", "pool", "sp"
ENGS = [PE, ACT, DVE, POOL, SP]

D = 1024
S = 4096
DEPTH = 4
NCH = 8
T = 512
NTB = S // T
FF = 2816
NF = FF // 128
QR = 384
KVR = 256
NH = 8
ROPE = 64
WINDOWS = (2, 4, 8, 16)
EPS = 1e-6
NEG = -30000.0
QSCALE = 192.0 ** -0.5


class _Op:
    __slots__ = ("eng", "fn", "deps", "is_dma", "key", "sig", "needed")

    def __init__(self, eng, fn, is_dma, key):
        self.eng = eng
        self.fn = fn
        self.deps = []
        self.is_dma = is_dma
        self.key = key
        self.sig = None
        self.needed = False


class Prog:
    def __init__(self, nc):
        self.nc = nc
        self.ops = {e: [] for e in ENGS}
        self.last_w = {}
        self.readers = {}
        self.dma_cnt = {}
        self.pending_dma = []

    def _record(self, eng, fn, reads, writes, is_dma=False, key=None):
        op = _Op(eng, fn, is_dma, key)
        deps = {}
        for t in reads:
            w = self.last_w.get(t)
            if w is not None:
                deps[id(w)] = w
        for t in writes:
            w = self.last_w.get(t)
            if w is not None:
                deps[id(w)] = w
            for r in self.readers.get(t, {}).values():
                deps[id(r)] = r
        for d in deps.values():
            if d is op:
                continue
            if (not d.is_dma) and (not is_dma) and d.eng == PE and eng == PE:
                continue
            op.deps.append(d)
            d.needed = True
        for t in writes:
            self.last_w[t] = op
            self.readers[t] = {}
        rk = (eng, id(op)) if is_dma else eng
        for t in reads:
            if t not in writes:
                self.readers.setdefault(t, {})[rk] = op
        self.ops[eng].append(op)
        if is_dma:
            self.pending_dma.append(op)
        return op

    def pe(self, fn, reads=(), writes=()):
        return self._record(PE, fn, reads, writes)

    def act(self, fn, reads=(), writes=()):
        return self._record(ACT, fn, reads, writes)

    def dve(self, fn, reads=(), writes=()):
        return self._record(DVE, fn, reads, writes)

    def pool(self, fn, reads=(), writes=()):
        return self._record(POOL, fn, reads, writes)

    def on(self, eng, fn, reads=(), writes=()):
        return self._record(eng, fn, reads, writes)

    def dma(self, eng, fn, reads=(), writes=(), key=None):
        return self._record(eng, fn, reads, writes, is_dma=True, key=key)

    def barrier(self):
        lasts = []
        for e in ENGS:
            for op in reversed(self.ops[e]):
                if op.fn is not None and not op.is_dma:
                    lasts.append(op)
                    break
        dmas = list(self.pending_dma)
        self.pending_dma = []
        for e in ENGS:
            b = _Op(e, None, False, None)
            for d in lasts + dmas:
                if (not d.is_dma) and d.eng == e and e == PE:
                    continue
                b.deps.append(d)
                d.needed = True
            self.ops[e].append(b)
        self.last_w = {}
        self.readers = {}

    def emit(self):
        nc = self.nc
        for e in ENGS:
            cnt = 0
            for op in self.ops[e]:
                if op.is_dma:
                    c = self.dma_cnt.get(op.key, 0) + 16
                    self.dma_cnt[op.key] = c
                    op.sig = ("dma_" + op.key, c)
                elif op.needed and op.fn is not None:
                    cnt += 1
                    op.sig = ("eng_" + e, cnt)
        names = ["eng_" + e for e in ENGS] + ["dma_" + k for k in self.dma_cnt]
        with contextlib.ExitStack() as st:
            sems = {n: st.enter_context(nc.semaphore(n)) for n in names}
            block = st.enter_context(nc.Block())

            def make(e):
                def body(h):
                    seen = {}
                    for op in self.ops[e]:
                        for d in op.deps:
                            sn, v = d.sig
                            if seen.get(sn, 0) >= v:
                                continue
                            seen[sn] = v
                            h.wait_ge(sems[sn], v)
                        if op.fn is None:
                            continue
                        ins = op.fn(h)
                        if op.sig is not None:
                            ins.then_inc(sems[op.sig[0]], 16 if op.is_dma else 1)
                return body

            block.tensor(make(PE))
            block.scalar(make(ACT))
            block.vector(make(DVE))
            block.gpsimd(make(POOL))
            block.sync(make(SP))


class Arena:
    def __init__(self, nc, nbytes):
        self.t = nc.alloc_sbuf_tensor("arena", [128, nbytes // 2], BF16)
        self.nbytes = nbytes
        self.off = 0

    def alloc(self, shape, dtype, parts=128):
        esz = 4 if dtype in (F32, I32) else 2
        n = 1
        for s in shape:
            n *= s
        nb = (n * esz + 63) // 64 * 64
        assert self.off + nb <= self.nbytes, f"arena overflow {self.off}+{nb}>{self.nbytes}"
        v = self.t[0:parts, self.off // 2:(self.off + n * esz) // 2]
        if esz == 4:
            v = v.bitcast(dtype)
        self.off += nb
        if len(shape) == 2:
            v = v.rearrange("p (a b) -> p a b", a=shape[0])
        elif len(shape) == 3:
            v = v.rearrange("p (a b c) -> p a b c", a=shape[0], b=shape[1])
        return v

    def mark(self):
        return self.off

    def reset(self, m):
        self.off = m


def _vec_layout():
    cols = {}
    off = 0
    for name, n in [("cT", 8), ("mod_b", 4 * 48), ("n1g", 32), ("n2g", 32), ("pool_b", 16),
                    ("pool_s", 16), ("kvg", 8), ("ckvg", 2), ("qng", 6), ("convw", 4 * 3 * NF),
                    ("convb", 4 * NF), ("fg", 8), ("eps", 1), ("invf", 1), ("sgn", 1),
                    ("invc", 64), ("one", 1)]:
        cols[name] = off
        off += n
    return cols, off


VCOL, NV = _vec_layout()


def _chunks(v):
    v = np.asarray(v, np.float32)
    lead = v.shape[:-1]
    n = v.shape[-1] // 128
    v = v.reshape(lead + (n, 128))
    v = np.moveaxis(v, -1, 0)
    return np.ascontiguousarray(v.reshape(128, -1))


def _build_vecs(inp, b):
    V = np.zeros((128, NV), np.float32)

    def put(name, arr):
        V[:, VCOL[name]:VCOL[name] + arr.shape[1]] = arr

    put("cT", _chunks(inp["c"][b]))
    put("mod_b", _chunks(inp["mod_b"]))
    put("n1g", _chunks(inp["norm1_g"]))
    put("n2g", _chunks(inp["norm2_g"]))
    put("pool_b", _chunks(inp["pool_b"]))
    put("pool_s", _chunks(inp["pool_scale"]))
    put("kvg", _chunks(inp["kv_in_g"]))
    put("ckvg", _chunks(inp["ckv_norm_g"]))
    put("qng", _chunks(inp["q_norm_g"]))
    put("convw", _chunks(inp["conv_w"]))
    put("convb", _chunks(inp["conv_b"]))
    put("fg", _chunks(inp["final_g"]))
    V[:, VCOL["eps"]] = EPS
    V[:, VCOL["one"]] = 1.0
    inv = (1.0 / (10000.0 ** (np.arange(0, ROPE, 2, dtype=np.float32) / ROPE))).astype(np.float32)
    V[0:32, VCOL["invf"]] = inv
    V[32:64, VCOL["invf"]] = inv
    V[0:32, VCOL["sgn"]] = -1.0
    V[32:64, VCOL["sgn"]] = 1.0
    for g, w in enumerate(WINDOWS):
        for t in range(16):
            V[:, VCOL["invc"] + g * 16 + t] = 1.0 / min(t + 1, w)
    return V


def _consts():
    ident = np.eye(128, dtype=np.float32)
    q = np.arange(128)[:, None]
    k = np.arange(128)[None, :]
    maskq = np.where(k <= q, 0.0, NEG).astype(np.float32)
    maskt = np.ascontiguousarray(maskq.T)
    return np.concatenate([ident, maskq, maskt], axis=1)


def _maskg():
    mg = np.zeros((128, 4, 4, 128), np.float32)
    k = np.arange(128)[:, None]
    q = np.arange(128)[None, :]
    tri = np.where(k <= q, 0.0, NEG).astype(np.float32)
    for r in range(4):
        for ip in range(4):
            if ip < r:
                mg[:, r, ip, :] = NEG
            elif ip == r:
                mg[:, r, ip, :] = tri
    return np.ascontiguousarray(mg.reshape(128, 4 * 512))


def build_program(stop_after=None):
    nc = bass.Bass("TRN2", target_bir_lowering=False)
    dr = {}

    def din(name, shape, dt=F32):
        dr[name] = nc.dram_tensor(name, list(shape), dt, kind="ExternalInput").ap()
        return dr[name]

    x_d = din("x", [S, D])
    pos_d = din("pos", [1, S], I32)
    vecs_d = din("vecs", [128, NV])
    cst_d = din("cst", [128, 384])
    din("maskg", [128, 4 * T])
    mod_w = din("mod_w", [DEPTH, D, 6 * D])
    pool_w = din("pool_w", [2, 4, 256, 256])
    w_dkv = din("w_dkv", [D, KVR + ROPE])
    w_uk = din("w_uk", [KVR, NH * 128])
    w_uv = din("w_uv", [KVR, NH * 128])
    w_dq = din("w_dq", [2, D, QR])
    w_uq = din("w_uq", [2, QR, NH * 192])
    w_o = din("w_o", [2, NH * 128, D])
    w_up = din("w_up", [DEPTH, D, 2 * FF])
    w_down = din("w_down", [DEPTH, FF, D])
    y_d = nc.dram_tensor("y", [S, D], F32, kind="ExternalOutput").ap()
    cs_d = nc.dram_tensor("cs_scr", [2, 64, S], F32).ap()
    ckv_d = nc.dram_tensor("ckv_scr", [128, 2, S], BF16).ap()
    kr_d = nc.dram_tensor("kr_scr", [64, S], BF16).ap()
    cq_d = nc.dram_tensor("cq_scr", [128, 3, S], BF16).ap()
    oT_d = nc.dram_tensor("oT_scr", [128, NH, S], BF16).ap()

    A = Arena(nc, 211200)
    P = Prog(nc)
    ps = [nc.alloc_psum_tensor(f"ps{i}", [128, 512], F32) for i in range(8)]

    xT = A.alloc([NCH, S], F32)
    vecs = A.alloc([NV], F32)
    cst = A.alloc([384], F32)
    mods = A.alloc([DEPTH * 48], F32)
    coef = A.alloc([DEPTH * 2 * 8], F32)
    pcoef = A.alloc([2 * 2 * 8], F32)
    identb = A.alloc([128], BF16)
    onesb = A.alloc([128], BF16)
    maskqb = A.alloc([128], BF16)
    masktb = A.alloc([128], BF16)
    siluc = A.alloc([8], BF16)
    identf = cst[:, 0:128]
    PH = A.mark()

    def vc(name, i=0, n=1, parts=128):
        o = VCOL[name] + i
        return vecs[0:parts, o:o + n]

    def mcol(l, k, c):
        o = l * 48 + k * 8 + c
        return mods[:, o:o + 1]

    def acol(l, which, c):
        o = (l * 2 + which) * 8 + c
        return coef[:, o:o + 1]

    def xtok(c, t0, t1):
        return [f"x{c}_{b}" for b in range(t0 // T, (t1 + T - 1) // T)]

    eps_col = vc("eps")

    P.dma(SP, lambda e: e.dma_start(out=vecs, in_=vecs_d), writes=["vecs"], key="c0")
    P.dma(SP, lambda e: e.dma_start(out=cst, in_=cst_d), writes=["cst"], key="c1")
    P.dve(lambda e: e.tensor_copy(identb, cst[:, 0:128]), reads=["cst"], writes=["identb"])
    P.dve(lambda e: e.tensor_copy(maskqb, cst[:, 128:256]), reads=["cst"], writes=["maskqb"])
    P.dve(lambda e: e.tensor_copy(masktb, cst[:, 256:384]), reads=["cst"], writes=["masktb"])
    P.dve(lambda e: e.memset(onesb, 1.0), writes=["onesb"])
    P.act(lambda e: e.activation(siluc, vc("cT", 0, 8), AF.Silu), reads=["vecs"], writes=["siluc"])

    xin = [A.alloc([4, D], F32) for _ in range(2)]
    for tg in range(NTB):
        xb = xin[tg % 2]
        P.dma(SP, lambda e, xb=xb, tg=tg: e.dma_start(
            out=xb, in_=x_d[tg * T:(tg + 1) * T, :].rearrange("(i p) d -> p i d", p=128)),
            writes=[f"xin{tg % 2}"], key=f"xin{tg % 2}")
        for c in range(NCH):
            bank = ps[c % 4]
            for i in range(4):
                P.pe(lambda e, bank=bank, xb=xb, i=i, c=c: e.transpose(
                    bank[:, i * 128:(i + 1) * 128], xb[:, i, c * 128:(c + 1) * 128], identf),
                    reads=[f"xin{tg % 2}", "cst"], writes=[f"ps{c % 4}"])
            dst = xT[:, c, tg * T:(tg + 1) * T]
            if c % 2 == 0:
                P.dve(lambda e, dst=dst, bank=bank: e.tensor_copy(dst, bank[:]),
                      reads=[f"ps{c % 4}"], writes=xtok(c, tg * T, (tg + 1) * T))
            else:
                P.act(lambda e, dst=dst, bank=bank: e.activation(dst, bank[:], AF.Copy),
                      reads=[f"ps{c % 4}"], writes=xtok(c, tg * T, (tg + 1) * T))

    def mods_pieces(l, bank, btok, mw=None, wcols=512):
        if mw is None:
            mw = [A.alloc([8, wcols], BF16) for _ in range(2)]
        npc = wcols // 128
        out = []
        for q in range(6 * D // wcols):
            def piece(q=q):
                buf = mw[q % 2]
                mtok = f"mw{q % 2}"
                P.dma(POOL, lambda e: e.dma_start(
                    out=buf, in_=mod_w[l][:, q * wcols:(q + 1) * wcols].rearrange("(kc p) n -> p kc n", p=128)),
                    writes=[mtok], key=mtok)
                for jj in range(npc):
                    j = q * npc + jj
                    for kc in range(8):
                        P.pe(lambda e, jj=jj, kc=kc, j=j: e.matmul(
                            bank[:, j:j + 1], buf[:, kc, jj * 128:(jj + 1) * 128], siluc[:, kc:kc + 1],
                            start=(kc == 0), stop=(kc == 7)),
                            reads=[mtok, "siluc"], writes=[btok])
            out.append(piece)

        def fin():
            P.dve(lambda e: e.tensor_tensor(mods[:, l * 48:(l + 1) * 48], bank[:, 0:48], vc("mod_b", l * 48, 48),
                                            ALU.add), reads=[btok, "vecs"], writes=["mods"])
            for which, (gname, k) in enumerate([("n1g", 1), ("n2g", 4)]):
                dst = coef[:, (l * 2 + which) * 8:(l * 2 + which) * 8 + 8]
                src = mods[:, l * 48 + k * 8:l * 48 + k * 8 + 8]
                P.dve(lambda e, dst=dst, src=src, gname=gname: e.scalar_tensor_tensor(
                    dst, src, 1.0, vc(gname, l * 8, 8), ALU.add, ALU.mult),
                    reads=["mods", "vecs"], writes=["coef"])
            if l < 2:
                pc = pcoef[:, (l * 2) * 8:(l * 2) * 8 + 8]
                pb = pcoef[:, (l * 2 + 1) * 8:(l * 2 + 1) * 8 + 8]
                g1 = mods[:, l * 48 + 16:l * 48 + 24]
                P.dve(lambda e: e.tensor_tensor(pc, g1, vc("pool_s", l * 8, 8), ALU.mult),
                      reads=["mods", "vecs"], writes=["pcoef"])
                P.dve(lambda e: e.tensor_tensor(pb, pc, vc("pool_b", l * 8, 8), ALU.mult),
                      reads=["pcoef", "vecs"], writes=["pcoef"])
        out.append(fin)
        return out

    def mods_ops(l, bank, btok, mw=None, wcols=512):
        for f_ in mods_pieces(l, bank, btok, mw, wcols):
            f_()

    mw_init = [A.alloc([8, 512], BF16) for _ in range(2)]
    mods_ops(0, ps[5], "ps5", mw_init)
    mods_ops(1, ps[4], "ps4", mw_init)

    HS = S // 8
    posf = A.alloc([HS], F32, parts=64)
    ang = A.alloc([HS], F32, parts=64)
    kf = A.alloc([HS], F32, parts=64)
    ki = A.alloc([HS], I32, parts=64)
    rr = A.alloc([HS], F32, parts=64)
    C1 = 6.28125
    C2 = float(2 * np.pi - 6.28125)
    for hf in range(S // HS):
        P.dma(SP, lambda e, hf=hf: e.dma_start(
            out=ki, in_=pos_d[:, hf * HS:(hf + 1) * HS].partition_broadcast(64)),
            writes=["ki"], key="pos")
        P.dve(lambda e: e.tensor_copy(posf, ki), reads=["ki"], writes=["posf"])
        P.dve(lambda e: e.tensor_scalar(ang, posf, vc("invf", 0, 1, 64), None, ALU.mult),
              reads=["posf", "vecs"], writes=["ang"])
        for which, shift in enumerate([float(np.pi / 2), 0.0]):
            P.dve(lambda e, shift=shift: e.tensor_scalar(rr, ang, shift, None, ALU.add),
                  reads=["ang"], writes=["rr"])
            P.dve(lambda e: e.tensor_scalar(kf, rr, float(1 / (2 * np.pi)), None, ALU.mult),
                  reads=["rr"], writes=["kf"])
            P.dve(lambda e: e.tensor_copy(ki, kf), reads=["kf"], writes=["ki"])
            P.dve(lambda e: e.tensor_copy(kf, ki), reads=["ki"], writes=["kf"])
            P.dve(lambda e: e.scalar_tensor_tensor(rr, kf, -C1, rr, ALU.mult, ALU.add),
                  reads=["kf", "rr"], writes=["rr"])
            P.dve(lambda e: e.scalar_tensor_tensor(rr, kf, -C2, rr, ALU.mult, ALU.add),
                  reads=["kf", "rr"], writes=["rr"])
            P.dve(lambda e: e.tensor_scalar(kf, rr, float(np.pi), float(-2 * np.pi), ALU.is_gt, ALU.mult),
                  reads=["rr"], writes=["kf"])
            P.dve(lambda e: e.tensor_tensor(rr, rr, kf, ALU.add), reads=["kf", "rr"], writes=["rr"])
            P.dve(lambda e: e.tensor_scalar(kf, rr, float(-np.pi), float(2 * np.pi), ALU.is_lt, ALU.mult),
                  reads=["rr"], writes=["kf"])
            P.dve(lambda e: e.tensor_tensor(rr, rr, kf, ALU.add), reads=["kf", "rr"], writes=["rr"])
            if which == 0:
                P.act(lambda e: e.activation(rr, rr, AF.Sin), reads=["rr"], writes=["rr"])
            else:
                P.act(lambda e: e.activation(rr, rr, AF.Sin, scale=vc("sgn", 0, 1, 64)), reads=["rr", "vecs"],
                      writes=["rr"])
            P.dma(SP, lambda e, which=which, hf=hf: e.dma_start(
                out=cs_d[which, :, hf * HS:(hf + 1) * HS], in_=rr), reads=["rr"], writes=["cs_d"], key="cso")
    P.barrier()
    A.reset(PH)

    def rstd_block(srcs, dim, sq, ss_bank, ss_tok, rs, rstd, rtok, src_reads, tg=""):
        nsrc = len(srcs)
        nb = len(sq)
        for i, src in enumerate(srcs):
            P.act(lambda e, src=src, i=i: e.activation(sq[i % nb], src, AF.Square),
                  reads=src_reads[i], writes=[f"sq{i % nb}{tg}"])
            P.pe(lambda e, i=i: e.matmul(ss_bank[:], onesb, sq[i % nb], start=(i == 0), stop=(i == nsrc - 1)),
                 reads=[f"sq{i % nb}{tg}", "onesb"], writes=[ss_tok])
        P.act(lambda e: e.activation(rs, ss_bank[:], AF.Ln, bias=eps_col, scale=1.0 / dim),
              reads=[ss_tok, "vecs"], writes=["rs" + tg])
        P.act(lambda e: e.activation(rstd, rs, AF.Exp, scale=-0.5), reads=["rs" + tg], writes=[rtok])

    def dump_x(normed):
        m = A.mark()
        sq = [A.alloc([T], BF16) for _ in range(8)]
        rs = A.alloc([T], F32)
        rstd = A.alloc([T], F32)
        o32 = A.alloc([NCH, T], F32)
        stage = A.alloc([4, D], F32)
        for tb in range(NTB):
            t0, t1 = tb * T, (tb + 1) * T
            if normed:
                rstd_block([xT[:, c, t0:t1] for c in range(NCH)], D, sq, ps[7], "ps7", rs, rstd, "rstd",
                           [xtok(c, t0, t1) for c in range(NCH)])
                for c in range(NCH):
                    P.dve(lambda e, c=c, t0=t0, t1=t1: e.scalar_tensor_tensor(
                        o32[:, c, :], xT[:, c, t0:t1], vc("fg", c), rstd, ALU.mult, ALU.mult),
                        reads=xtok(c, t0, t1) + ["rstd", "vecs"], writes=[f"o32_{c}"])
            for i in range(4):
                for hh in range(2):
                    bank = ps[(i * 2 + hh) % 4]
                    btok = f"ps{(i * 2 + hh) % 4}"
                    for cc in range(4):
                        c = hh * 4 + cc
                        src = o32[:, c, i * 128:(i + 1) * 128] if normed else xT[:, c, t0 + i * 128:t0 + (i + 1) * 128]
                        rd = [f"o32_{c}"] if normed else xtok(c, t0, t1)
                        P.pe(lambda e, bank=bank, src=src, cc=cc: e.transpose(
                            bank[:, cc * 128:(cc + 1) * 128], src, identf), reads=rd + ["cst"], writes=[btok])
                    dst = stage[:, i, hh * 512:(hh + 1) * 512]
                    if hh == 0:
                        P.dve(lambda e, dst=dst, bank=bank: e.tensor_copy(dst, bank[:]), reads=[btok],
                              writes=[f"stage{i}"])
                    else:
                        P.act(lambda e, dst=dst, bank=bank: e.activation(dst, bank[:], AF.Copy), reads=[btok],
                              writes=[f"stage{i}"])
            P.dma(SP, lambda e, t0=t0, t1=t1: e.dma_start(
                out=y_d[t0:t1, :].rearrange("(i p) d -> p i d", p=128), in_=stage),
                reads=[f"stage{i}" for i in range(4)], writes=["y"], key="yout")
        P.barrier()
        A.reset(m)

    def finish():
        P.emit()
        return nc

    def ffn_phase(l):
        m = A.mark()
        NB = 3
        sq = [A.alloc([T], BF16) for _ in range(2)]
        hT = A.alloc([NCH, T], BF16)
        u = A.alloc([NF, T], BF16)
        NW = 4
        wup = [A.alloc([8, 2, 128], BF16) for _ in range(NW)]
        wd = [A.alloc([NF, 128], BF16) for _ in range(2)]
        a_sb = [A.alloc([T + 2], F32) for _ in range(NB)]
        t1b = [A.alloc([T], F32) for _ in range(NB)]
        halo = A.alloc([NF, 2], F32)
        tmp, tmptok = t1b[0], "t10"
        rs, rstok = t1b[1], "t11"
        rstd = rs
        P.dve(lambda e: e.memset(halo, 0.0), writes=[f"halo{j}" for j in range(NF)])
        wup_src = w_up[l].rearrange("(kc p) (h j m) -> p kc h j m", p=128, h=2, m=128)
        wd_src = w_down[l].rearrange("(j p) (c m) -> p j c m", p=128, m=128)

        up_list = [(tb, j) for tb in range(NTB) for j in range(NF)]
        dn_list = [(tb, c) for tb in range(NTB) for c in range(NCH)]
        st = {"up": 0, "dn": 0}

        def issue_up(upto):
            while st["up"] < min(upto, len(up_list)):
                n = st["up"]
                _, j = up_list[n]
                wb = wup[n % NW]
                wtok = f"wup{n % NW}"
                P.dma(POOL, lambda e, wb=wb, j=j: e.dma_start(out=wb[:, :, 0, :], in_=wup_src[:, :, 0, j, :]),
                      writes=[wtok + "a"], key=wtok + "a")
                P.dma(POOL, lambda e, wb=wb, j=j: e.dma_start(out=wb[:, :, 1, :], in_=wup_src[:, :, 1, j, :]),
                      writes=[wtok + "v"], key=wtok + "v")
                st["up"] += 1

        def issue_dn(upto):
            while st["dn"] < min(upto, len(dn_list)):
                n = st["dn"]
                _, c = dn_list[n]
                wdb = wd[n % 2]
                P.dma(POOL, lambda e, wdb=wdb, c=c: e.dma_start(out=wdb, in_=wd_src[:, :, c, :]),
                      writes=[f"wd{n % 2}"], key=f"wd{n % 2}")
                st["dn"] += 1

        def stats_h(tb):
            t0, t1 = tb * T, (tb + 1) * T
            srcs = [xT[:, c, t0:t1] for c in range(NCH)]
            for i, src in enumerate(srcs):
                P.act(lambda e, src=src, i=i: e.activation(sq[i % 2], src, AF.Square),
                      reads=xtok(i, t0, t1), writes=[f"sq{i % 2}"])
                P.pe(lambda e, i=i: e.matmul(ps[0][:], onesb, sq[i % 2], start=(i == 0), stop=(i == NCH - 1)),
                     reads=[f"sq{i % 2}", "onesb"], writes=["ps0"])
            P.act(lambda e: e.activation(rs, ps[0][:], AF.Ln, bias=eps_col, scale=1.0 / D),
                  reads=["ps0", "vecs"], writes=[rstok])
            P.act(lambda e: e.activation(rstd, rs, AF.Exp, scale=-0.5), reads=[rstok], writes=[rstok])
            for c in range(NCH):
                tm, tmt = (t1b[0], "t10") if c % 2 == 0 else (t1b[2], "t12")
                P.dve(lambda e, c=c, t0=t0, t1=t1, tm=tm: e.tensor_tensor(tm, xT[:, c, t0:t1], rstd, ALU.mult),
                      reads=xtok(c, t0, t1) + [rstok], writes=[tmt])
                P.act(lambda e, c=c, tm=tm: e.activation(hT[:, c, :], tm, AF.Identity,
                                                         bias=mcol(l, 3, c), scale=acol(l, 1, c)),
                      reads=[tmt, "mods", "coef"], writes=[f"hT{c}"])

        def down(tb, c):
            t0, t1 = tb * T, (tb + 1) * T
            n = tb * NCH + c
            issue_dn(n + 2)
            wdb = wd[n % 2]
            wdtok = f"wd{n % 2}"
            yb = ps[6 + c % 2]
            ytok = f"ps{6 + c % 2}"
            for j in range(NF):
                P.pe(lambda e, yb=yb, wdb=wdb, j=j: e.matmul(yb[:], wdb[:, j, :], u[:, j, :],
                                                              start=(j == 0), stop=(j == NF - 1)),
                     reads=[wdtok, f"u{j}"], writes=[ytok])
            P.dve(lambda e, yb=yb, c=c, t0=t0, t1=t1: e.scalar_tensor_tensor(
                xT[:, c, t0:t1], yb[:], mcol(l, 5, c), xT[:, c, t0:t1], ALU.mult, ALU.add),
                reads=[ytok, "mods"] + xtok(c, t0, t1), writes=xtok(c, t0, t1))

        issue_up(NW - 1)
        stats_h(0)
        pend2 = [None]

        def stage2_flush():
            if pend2[0] is not None:
                pend2[0]()
                pend2[0] = None

        for tb in range(NTB):
            for j in range(NF):
                n = tb * NF + j
                issue_up(n + NW)
                if j == NF - 6:
                    issue_dn(tb * NCH + 2)
                wb = wup[n % NW]
                wtok = f"wup{n % NW}"
                ab, vb = ps[n % NB], ps[3 + n % NB]
                atok, vtok = f"ps{n % NB}", f"ps{3 + n % NB}"
                for kc in range(8):
                    P.pe(lambda e, ab=ab, wb=wb, kc=kc: e.matmul(ab[:], wb[:, kc, 0, :], hT[:, kc, :],
                                                                  start=(kc == 0), stop=(kc == 7)),
                         reads=[wtok + "a", f"hT{kc}"], writes=[atok])
                for kc in range(8):
                    P.pe(lambda e, vb=vb, wb=wb, kc=kc: e.matmul(vb[:], wb[:, kc, 1, :], hT[:, kc, :],
                                                                  start=(kc == 0), stop=(kc == 7)),
                         reads=[wtok + "v", f"hT{kc}"], writes=[vtok])
                asb = a_sb[n % NB]
                t1_ = t1b[n % NB]
                astok, ttok = f"asb{n % NB}", f"t1{n % NB}"
                cw = lambda k, j=j: vc("convw", (l * 3 + k) * NF + j)
                cb = vc("convb", l * NF + j)
                P.dve(lambda e, asb=asb, j=j: e.tensor_copy(asb[:, 0:2], halo[:, j, :]),
                      reads=[f"halo{j}"], writes=[astok + "h"])
                P.act(lambda e, asb=asb, ab=ab: e.activation(asb[:, 2:T + 2], ab[:], AF.Copy),
                      reads=[atok], writes=[astok])
                P.act(lambda e, t1_=t1_, ab=ab, cw=cw, cb=cb: e.activation(t1_, ab[:], AF.Identity,
                                                                          bias=cb, scale=cw(2)),
                      reads=[atok, "vecs"], writes=[ttok])
                stage2_flush()
                P.dve(lambda e, t1_=t1_, asb=asb, cw=cw: e.scalar_tensor_tensor(
                    t1_, asb[:, 1:T + 1], cw(1), t1_, ALU.mult, ALU.add),
                    reads=[astok, astok + "h", ttok, "vecs"], writes=[ttok])
                P.dve(lambda e, t1_=t1_, asb=asb, cw=cw: e.scalar_tensor_tensor(
                    t1_, asb[:, 0:T], cw(0), t1_, ALU.mult, ALU.add),
                    reads=[astok, astok + "h", ttok, "vecs"], writes=[ttok])
                P.dve(lambda e, asb=asb, j=j: e.tensor_copy(halo[:, j, :], asb[:, T:T + 2]),
                      reads=[astok], writes=[f"halo{j}"])

                def stage2(t1_=t1_, vb=vb, j=j, ttok=ttok, vtok=vtok):
                    P.act(lambda e: e.activation(t1_, t1_, AF.Gelu), reads=[ttok], writes=[ttok])
                    P.dve(lambda e: e.tensor_tensor(u[:, j, :], t1_, vb[:], ALU.mult),
                          reads=[ttok, vtok], writes=[f"u{j}"])
                pend2[0] = stage2
            stage2_flush()
            down(tb, 0)
            if tb + 1 < NTB:
                stats_h(tb + 1)
            for c in range(1, NCH):
                down(tb, c)
        P.barrier()
        A.reset(m)

    def pool_phase(l):
        m = A.mark()
        sq = [A.alloc([T], BF16) for _ in range(4)]
        rs = A.alloc([T], F32)
        rstd_all = A.alloc([S], F32)
        HW = S // 2
        hp = A.alloc([16 + HW], F32)
        sA = A.alloc([16 + HW], F32)
        sB = A.alloc([16 + HW], F32)
        pooled = A.alloc([2, S], BF16)
        pw = A.alloc([4, 2, 256], BF16)
        ytmp = [A.alloc([T], F32) for _ in range(2)]
        fix = A.alloc([16], F32)
        pwf = sB[:, 0:2048].rearrange("p (g k n) -> p g k n", g=4, k=2)
        P.dma(SP, lambda e: e.dma_start(out=pwf, in_=pool_w[l].rearrange("g (kc p) n -> p g kc n", p=128)),
              writes=["sB"], key="pw")
        for g_ in range(4):
            for kc_ in range(2):
                P.dve(lambda e, g_=g_, kc_=kc_: e.tensor_scalar(
                    pw[:, g_, kc_, :], pwf[:, g_, kc_, :], acol(l, 0, 2 * g_ + kc_), None, ALU.mult),
                    reads=["sB", "coef"], writes=["pw"])
        for tb in range(NTB):
            t0, t1 = tb * T, (tb + 1) * T
            rstd_block([xT[:, c, t0:t1] for c in range(NCH)], D, sq, ps[6], "ps6", rs, rstd_all[:, t0:t1],
                       f"rstd{tb}", [xtok(c, t0, t1) for c in range(NCH)])
        P.dve(lambda e: e.memset(hp[:, 0:16], 0.0), writes=["hp"])
        P.dve(lambda e: e.memset(sA[:, 0:16], 0.0), writes=["sA"])
        P.dve(lambda e: e.memset(sB[:, 0:16], 0.0), writes=["sB"])
        rall = [f"rstd{tb}" for tb in range(NTB)]
        ny = 0
        for g in range(4):
            w = WINDOWS[g]
            for mo in range(2):
                c = 2 * g + mo
                for hf in range(2):
                    base = hf * HW
                    lo = 16 if hf == 0 else 0
                    tlo = base - 16 + lo
                    ncol = 16 + HW - lo
                    xt = xtok(c, max(tlo, 0), base + HW)
                    if hf == 0:
                        P.dve(lambda e: e.memset(hp[:, 0:16], 0.0), reads=["hp"], writes=["hp"])
                    P.dve(lambda e, lo=lo, tlo=tlo, ncol=ncol, c=c: e.tensor_tensor(
                        hp[:, lo:lo + ncol], xT[:, c, tlo:tlo + ncol], rstd_all[:, tlo:tlo + ncol], ALU.mult),
                        reads=xt + rall, writes=["hp"])
                    src, stok = hp, "hp"
                    bufs = [(sA, "sA"), (sB, "sB")]
                    sh = 1
                    k = 0
                    W_ = 16 + HW
                    while sh < w:
                        dstb, dtok = bufs[k % 2]
                        first = 2 * sh - 1
                        P.dve(lambda e, dstb=dstb, src=src, sh=sh, first=first: e.tensor_tensor(
                            dstb[:, first:W_], src[:, first:W_], src[:, first - sh:W_ - sh], ALU.add),
                            reads=[stok], writes=[dtok])
                        src, stok = dstb, dtok
                        sh *= 2
                        k += 1
                    P.dve(lambda e, src=src, mo=mo, base=base, w=w: e.scalar_tensor_tensor(
                        pooled[:, mo, base:base + HW], src[:, 16:16 + HW], 1.0 / w, hp[:, 16:16 + HW],
                        ALU.mult, ALU.subtract),
                        reads=[stok, "hp"], writes=[f"pooled{mo}_{hf}"])
                    if hf == 0:
                        P.dve(lambda e, src=src, g=g: e.tensor_tensor(
                            fix, src[:, 16:32], vc("invc", g * 16, 16), ALU.mult),
                            reads=[stok, "vecs"], writes=["fix"])
                        P.dve(lambda e, mo=mo: e.tensor_tensor(
                            pooled[:, mo, 0:16], fix, hp[:, 16:32], ALU.subtract),
                            reads=["fix", "hp", f"pooled{mo}_0"], writes=[f"pooled{mo}_0"])
            for tb in range(NTB):
                t0, t1 = tb * T, (tb + 1) * T
                hfb = tb // (NTB // 2)
                for mo in range(2):
                    c = 2 * g + mo
                    yb = ps[ny % 2]
                    ytok = f"ps{ny % 2}"
                    yt = ytmp[ny % 2]
                    yttok = f"ytmp{ny % 2}"
                    ny += 1
                    for kc in range(2):
                        P.pe(lambda e, yb=yb, kc=kc, mo=mo, g=g, t0=t0, t1=t1: e.matmul(
                            yb[:], pw[:, g, kc, mo * 128:(mo + 1) * 128], pooled[:, kc, t0:t1],
                            start=(kc == 0), stop=(kc == 1)),
                            reads=["pw", f"pooled{kc}_{hfb}"], writes=[ytok])
                    P.act(lambda e, yb=yb, yt=yt, c=c: e.activation(
                        yt, yb[:], AF.Identity, bias=pcoef[:, (l * 2 + 1) * 8 + c:(l * 2 + 1) * 8 + c + 1],
                        scale=pcoef[:, (l * 2) * 8 + c:(l * 2) * 8 + c + 1]),
                        reads=[ytok, "pcoef"], writes=[yttok])
                    P.pool(lambda e, yt=yt, c=c, t0=t0, t1=t1: e.tensor_tensor(
                        xT[:, c, t0:t1], xT[:, c, t0:t1], yt, ALU.add),
                        reads=[yttok] + xtok(c, t0, t1), writes=xtok(c, t0, t1))
        P.barrier()
        A.reset(m)

    def kv_phase():
        m = A.mark()
        sq = [A.alloc([T], BF16) for _ in range(8)]
        rs = A.alloc([T], F32)
        rstd = A.alloc([T], F32)
        hT = A.alloc([NCH, T], BF16)
        wkv = A.alloc([8, KVR + 2 * ROPE], BF16)
        ckvb = A.alloc([2, T], BF16)
        csb = A.alloc([2, T], F32, parts=64)
        t1_ = A.alloc([T], F32, parts=64)
        t2_ = A.alloc([T], F32, parts=64)
        krb = A.alloc([T], BF16, parts=64)
        src = w_dkv.rearrange("(kc p) n -> p kc n", p=128)
        P.dma(POOL, lambda e: e.dma_start(out=wkv[:, :, 0:320], in_=src), writes=["wkv"], key="wkv")
        P.dma(POOL, lambda e: e.dma_start(out=wkv[:, :, 320:352], in_=src[:, :, 288:320]), writes=["wkv"], key="wkv")
        P.dma(POOL, lambda e: e.dma_start(out=wkv[:, :, 352:384], in_=src[:, :, 256:288]), writes=["wkv"], key="wkv")
        mpieces = mods_pieces(2, ps[5], "ps5")
        for tb in range(NTB):
            t0, t1 = tb * T, (tb + 1) * T
            rstd_block([xT[:, c, t0:t1] for c in range(NCH)], D, sq, ps[6], "ps6", rs, rstd, "rstd",
                       [xtok(c, t0, t1) for c in range(NCH)])
            for c in range(NCH):
                P.dve(lambda e, c=c, t0=t0, t1=t1: e.scalar_tensor_tensor(
                    hT[:, c, :], xT[:, c, t0:t1], vc("kvg", c), rstd, ALU.mult, ALU.mult),
                    reads=xtok(c, t0, t1) + ["rstd", "vecs"], writes=[f"hT{c}"])
            P.dma(SP, lambda e, t0=t0, t1=t1: e.dma_start(out=csb, in_=cs_d[:, :, t0:t1].rearrange("w p t -> p w t")),
                  writes=["csb"], key="csb")
            outs = [(ps[0], "ps0", 0, 128, 128), (ps[1], "ps1", 128, 256, 128),
                    (ps[2], "ps2", 256, 320, 64), (ps[3], "ps3", 320, 384, 64)]
            for bank, btok, c0, c1, mm in outs:
                for kc in range(8):
                    P.pe(lambda e, bank=bank, c0=c0, c1=c1, mm=mm, kc=kc: e.matmul(
                        bank[0:mm, :], wkv[:, kc, c0:c1], hT[:, kc, :], start=(kc == 0), stop=(kc == 7)),
                        reads=["wkv", f"hT{kc}"], writes=[btok])
            rstd_block([ps[0][:], ps[1][:]], KVR, sq, ps[7], "ps7", rs, rstd, "rstd", [["ps0"], ["ps1"]])
            for mi in range(2):
                P.dve(lambda e, mi=mi: e.scalar_tensor_tensor(
                    ckvb[:, mi, :], ps[mi][:], vc("ckvg", mi), rstd, ALU.mult, ALU.mult),
                    reads=[f"ps{mi}", "rstd", "vecs"], writes=["ckvb"])
            P.dma(SP, lambda e, t0=t0, t1=t1: e.dma_start(out=ckv_d[:, :, t0:t1], in_=ckvb),
                  reads=["ckvb"], writes=["ckv_d"], key="ckvo")
            P.dve(lambda e: e.tensor_tensor(t1_, ps[2][0:64, :], csb[:, 0, :], ALU.mult),
                  reads=["ps2", "csb"], writes=["t1"])
            P.dve(lambda e: e.tensor_tensor(t2_, ps[3][0:64, :], csb[:, 1, :], ALU.mult),
                  reads=["ps3", "csb"], writes=["t2"])
            P.dve(lambda e: e.tensor_tensor(krb, t1_, t2_, ALU.add), reads=["t1", "t2"], writes=["krb"])
            P.dma(SP, lambda e, t0=t0, t1=t1: e.dma_start(out=kr_d[:, t0:t1], in_=krb),
                  reads=["krb"], writes=["kr_d"], key="kro")
            for _ in range(2):
                if mpieces:
                    mpieces.pop(0)()
        while mpieces:
            mpieces.pop(0)()
        P.barrier()
        A.reset(m)

    def mla_q_phase(l):
        jl = l - 2
        m = A.mark()
        B2 = []
        sq8 = [A.alloc([T], BF16) for _ in range(8)]
        tmp8 = [A.alloc([T], F32) for _ in range(8)]
        for par in range(2):
            B2.append(dict(sq=sq8, rs=A.alloc([T], F32), rstd=A.alloc([T], F32),
                           tmp=tmp8, hT=A.alloc([NCH, T], BF16), cqb=A.alloc([3, T], BF16)))
        wdq = A.alloc([8, QR], BF16)
        P.dma(POOL, lambda e: e.dma_start(out=wdq, in_=w_dq[jl].rearrange("(kc p) n -> p kc n", p=128)),
              writes=["wdq"], key="wdq")
        mpieces = mods_pieces(3, ps[7], "ps7", None, 256) if l == 2 else []
        for tb in range(NTB):
            for _ in range(4):
                if mpieces:
                    mpieces.pop(0)()
            t0, t1 = tb * T, (tb + 1) * T
            par = tb % 2
            tg = f"_{par}"
            b = B2[par]
            sq, rs, rstd, tmp, hT, cqb = b["sq"], b["rs"], b["rstd"], b["tmp"], b["hT"], b["cqb"]
            pb = 3 * par
            rstd_block([xT[:, c, t0:t1] for c in range(NCH)], D, sq, ps[6], "ps6", rs, rstd, "rstd" + tg,
                       [xtok(c, t0, t1) for c in range(NCH)], "")
            for c in range(NCH):
                P.dve(lambda e, c=c, t0=t0, t1=t1, tmp=tmp, rstd=rstd: e.tensor_tensor(
                    tmp[c], xT[:, c, t0:t1], rstd, ALU.mult),
                    reads=xtok(c, t0, t1) + ["rstd" + tg], writes=[f"tmp{c}"])
                P.act(lambda e, c=c, tmp=tmp, hT=hT: e.activation(hT[:, c, :], tmp[c], AF.Identity,
                                                                  bias=mcol(l, 0, c), scale=acol(l, 0, c)),
                      reads=[f"tmp{c}", "mods", "coef"], writes=[f"hT{c}{tg}"])
            for mi in range(3):
                for kc in range(8):
                    P.pe(lambda e, mi=mi, kc=kc, hT=hT, pb=pb: e.matmul(
                        ps[pb + mi][:], wdq[:, kc, mi * 128:(mi + 1) * 128], hT[:, kc, :],
                        start=(kc == 0), stop=(kc == 7)),
                        reads=["wdq", f"hT{kc}{tg}"], writes=[f"ps{pb + mi}"])
            rstd_block([ps[pb][:], ps[pb + 1][:], ps[pb + 2][:]], QR, sq, ps[6], "ps6", rs, rstd, "rstd" + tg,
                       [[f"ps{pb}"], [f"ps{pb + 1}"], [f"ps{pb + 2}"]], "")
            for mi in range(3):
                P.dve(lambda e, mi=mi, cqb=cqb, rstd=rstd, pb=pb: e.scalar_tensor_tensor(
                    cqb[:, mi, :], ps[pb + mi][:], vc("qng", jl * 3 + mi), rstd, ALU.mult, ALU.mult),
                    reads=[f"ps{pb + mi}", "rstd" + tg, "vecs"], writes=["cqb" + tg])
            P.dma(SP, lambda e, t0=t0, t1=t1, cqb=cqb: e.dma_start(out=cq_d[:, :, t0:t1], in_=cqb),
                  reads=["cqb" + tg], writes=[f"cq_d{tb}"], key="cqo" + tg)
        while mpieces:
            mpieces.pop(0)()
        P.barrier()
        A.reset(m)

    def mla_attn_phase(l):
        jl = l - 2
        m = A.mark()
        qn = A.alloc([S], BF16)
        qra = A.alloc([S], BF16)
        kn = A.alloc([S], BF16)
        kra = A.alloc([S], BF16)
        vsb = A.alloc([S // 128, 128], BF16)
        og = [A.alloc([T], BF16) for _ in range(2)]
        cqbs = [A.alloc([3, T], BF16) for _ in range(2)]
        ckvbs = [A.alloc([2, T], BF16) for _ in range(2)]
        csb = A.alloc([2, T], F32)
        R1 = A.alloc([T], F32)
        R2 = A.alloc([3 * T], BF16)
        t1_ = R1[0:64, :]
        rl = R1
        t2_ = R2[:, 0:2 * T].bitcast(F32)[0:64, :]
        PT = [R2[:, 0:T], R2[:, T:2 * T], R2[:, 2 * T:3 * T]]
        wq = A.alloc([3, 256], BF16)
        wk = A.alloc([2, 128], BF16)
        wv = A.alloc([2, 128], BF16)
        maskg = A.alloc([4, T], BF16)
        mx = [A.alloc([8], F32) for _ in range(2)]
        mrow = [A.alloc([2], F32) for _ in range(2)]
        negm65 = [A.alloc([66], BF16) for _ in range(2)]
        P.dma(SP, lambda e: e.dma_start(out=kra[0:64, :], in_=kr_d), writes=["kr"], key="krl")
        P.dve(lambda e: e.memset(kra[64:65, :], 1.0), writes=["kr1"])
        for par in range(2):
            P.dve(lambda e, par=par: e.memset(negm65[par], 0.0), writes=[f"negm{par}"])
        P.dma(POOL, lambda e: e.dma_start(out=maskg, in_=dr["maskg"].rearrange("p (r q) -> p r q", r=4)),
              writes=["maskg"], key="maskg")
        wq_src = w_uq[jl].rearrange("(kc p) (h n) -> p kc h n", p=128, n=192)
        wk_src = w_uk.rearrange("(kc p) (h n) -> p kc h n", p=128, n=128)
        wv_src = w_uv.rearrange("(kc p) (h n) -> p kc h n", p=128, n=128)
        NQ = S // 128
        for h in range(NH):
            P.dma(POOL, lambda e, h=h: e.dma_start(out=wq[:, :, 0:192], in_=wq_src[:, :, h, :]),
                  writes=["wq"], key="wq")
            P.dma(POOL, lambda e, h=h: e.dma_start(out=wq[:, :, 192:224], in_=wq_src[:, :, h, 160:192]),
                  writes=["wq"], key="wq")
            P.dma(POOL, lambda e, h=h: e.dma_start(out=wq[:, :, 224:256], in_=wq_src[:, :, h, 128:160]),
                  writes=["wq"], key="wq")
            P.dma(POOL, lambda e, h=h: e.dma_start(out=wk, in_=wk_src[:, :, h, :]), writes=["wk"], key="wk")
            P.dma(POOL, lambda e, h=h: e.dma_start(out=wv, in_=wv_src[:, :, h, :]), writes=["wv"], key="wv")
            for tb in range(NTB):
                t0, t1 = tb * T, (tb + 1) * T
                cqb, ckvb = cqbs[tb % 2], ckvbs[tb % 2]
                cqtok, ckvtok = f"cqb{tb % 2}", f"ckvb{tb % 2}"
                P.dma(SP, lambda e, t0=t0, t1=t1, cqb=cqb: e.dma_start(out=cqb, in_=cq_d[:, :, t0:t1]),
                      writes=[cqtok], key=cqtok)
                P.dma(SP, lambda e, t0=t0, t1=t1, ckvb=ckvb: e.dma_start(out=ckvb, in_=ckv_d[:, :, t0:t1]),
                      writes=[ckvtok], key=ckvtok)
                P.dma(SP, lambda e, t0=t0, t1=t1: e.dma_start(
                    out=csb[0:64], in_=cs_d[:, :, t0:t1].rearrange("w p t -> p w t")), writes=["csb"], key="csb")
                for (bank, btok, c0, c1, mm) in [(ps[0], "ps0", 0, 128, 128), (ps[1], "ps1", 128, 192, 64),
                                                  (ps[2], "ps2", 192, 256, 64)]:
                    for kc in range(3):
                        P.pe(lambda e, bank=bank, c0=c0, c1=c1, mm=mm, kc=kc, cqb=cqb: e.matmul(
                            bank[0:mm, :], wq[:, kc, c0:c1], cqb[:, kc, :], start=(kc == 0), stop=(kc == 2)),
                            reads=["wq", cqtok], writes=[btok])
                for kc in range(2):
                    P.pe(lambda e, kc=kc, ckvb=ckvb: e.matmul(ps[3][:], wk[:, kc, :], ckvb[:, kc, :],
                                                   start=(kc == 0), stop=(kc == 1)),
                         reads=["wk", ckvtok], writes=["ps3"])
                P.act(lambda e, t0=t0, t1=t1: e.activation(qn[:, t0:t1], ps[0][:], AF.Identity, scale=QSCALE),
                      reads=["ps0"], writes=[f"qn{tb}"])
                P.act(lambda e, t0=t0, t1=t1: e.activation(kn[:, t0:t1], ps[3][:], AF.Copy),
                      reads=["ps3"], writes=[f"kn{tb}"])
                P.dve(lambda e: e.scalar_tensor_tensor(t1_, ps[1][0:64, :], QSCALE, csb[0:64, 0, :], ALU.mult, ALU.mult),
                      reads=["ps1", "csb"], writes=["R1"])
                P.dve(lambda e: e.scalar_tensor_tensor(t2_, ps[2][0:64, :], QSCALE, csb[0:64, 1, :], ALU.mult, ALU.mult),
                      reads=["ps2", "csb"], writes=["PT0", "PT1"])
                P.dve(lambda e, t0=t0, t1=t1: e.tensor_tensor(qra[0:64, t0:t1], t1_, t2_, ALU.add),
                      reads=["R1", "PT0", "PT1"], writes=[f"qr{tb}"])
                for i in range(4):
                    ti = tb * 4 + i
                    vb = ps[4 + i % 2]
                    vtok = f"ps{4 + i % 2}"
                    for kc in range(2):
                        P.pe(lambda e, vb=vb, kc=kc, i=i, ckvb=ckvb: e.matmul(
                            vb[:, 0:128], ckvb[:, kc, i * 128:(i + 1) * 128], wv[:, kc, :],
                            start=(kc == 0), stop=(kc == 1)), reads=["wv", ckvtok], writes=[vtok])
                    P.act(lambda e, vb=vb, ti=ti: e.activation(vsb[:, ti, :], vb[:, 0:128], AF.Copy),
                          reads=[vtok], writes=[f"v{ti}"])
            allk = [f"kn{tb}" for tb in range(NTB)] + ["kr", "kr1"]
            P.dve(lambda e: e.memset(qra[64:65, :], 0.0), writes=[f"qm{i}" for i in range(NQ)])
            p1n = [0]

            def pass1_chunks(i):
                par = i % 2
                G = i // 4
                nk = (i + 1) * 128
                nchunk = (nk + 511) // 512
                qtok = [f"qn{G}", f"qr{G}", f"qm{i}"]
                out = []
                for cidx in range(nchunk):
                    def chunk(cidx=cidx):
                        k0 = cidx * 512
                        k1 = min(nk, k0 + 512)
                        bank = ps[p1n[0] % 2]
                        btok = f"ps{p1n[0] % 2}"
                        p1n[0] += 1
                        isdiag = (cidx == nchunk - 1)
                        P.pe(lambda e: e.matmul(
                            bank[:, 0:k1 - k0], qn[:, i * 128:(i + 1) * 128], kn[:, k0:k1], start=True, stop=False),
                            reads=qtok + allk, writes=[btok])
                        P.pe(lambda e: e.matmul(
                            bank[:, 0:k1 - k0], qra[0:65, i * 128:(i + 1) * 128], kra[0:65, k0:k1], start=False,
                            stop=(not isdiag)), reads=qtok + allk, writes=[btok])
                        if isdiag:
                            P.pe(lambda e: e.matmul(
                                bank[:, k1 - k0 - 128:k1 - k0], identb, maskqb, start=False, stop=True),
                                reads=["identb", "maskqb"], writes=[btok])
                        P.dve(lambda e: e.reduce_max(mx[par][:, cidx:cidx + 1], bank[:, 0:k1 - k0], AX.X),
                              reads=[btok], writes=[f"mx{par}"])
                        if isdiag:
                            P.dve(lambda e: e.reduce_max(mrow[par][:, 0:1], mx[par][:, 0:nchunk], AX.X),
                                  reads=[f"mx{par}"], writes=[f"mrow{par}"])
                            P.dve(lambda e: e.tensor_scalar(negm65[par][:, 64:65], mrow[par][:, 0:1], -1.0, None,
                                                            ALU.mult),
                                  reads=[f"mrow{par}"], writes=[f"negm{par}"])
                    out.append(chunk)

                def tail():
                    P.pe(lambda e: e.matmul(ps[7][0:65, par * 128:(par + 1) * 128], negm65[par][:, 0:65],
                                            identb, start=True, stop=True),
                         reads=[f"negm{par}", "identb"], writes=["ps7"])
                    P.act(lambda e: e.activation(qra[64:65, i * 128:(i + 1) * 128],
                                                 ps[7][64:65, par * 128:(par + 1) * 128], AF.Copy),
                          reads=["ps7"], writes=[f"qm{i}"])
                return out, tail

            stn = [0]

            def pass2(G, fillers):
                g0, g1 = G * T, (G + 1) * T
                nkb = 4 * G + 4
                OT, ottok = ps[4], "ps4"
                Lb = ps[5]
                qtok = [f"qn{G}", f"qr{G}"] + [f"qm{4 * G + q}" for q in range(4)]
                pend = []
                for j in range(nkb):
                    sidx = stn[0] % 3
                    stn[0] += 1
                    bi = [2, 3, 6][sidx]
                    bank = ps[bi]
                    btok = f"ps{bi}"
                    pt = PT[sidx]
                    pttok = f"PT{sidx}"
                    c0 = max(0, j - 4 * G) * 128
                    P.pe(lambda e, bank=bank, j=j, c0=c0: e.matmul(
                        bank[:, c0:T], kn[:, j * 128:(j + 1) * 128], qn[:, g0 + c0:g1], start=True, stop=False),
                        reads=qtok + allk, writes=[btok])
                    P.pe(lambda e, bank=bank, j=j, c0=c0: e.matmul(
                        bank[:, c0:T], kra[0:65, j * 128:(j + 1) * 128], qra[0:65, g0 + c0:g1], start=False,
                        stop=(j < 4 * G)), reads=qtok + allk, writes=[btok])
                    if j >= 4 * G:
                        P.pe(lambda e, bank=bank, c0=c0: e.matmul(
                            bank[:, c0:c0 + 128], identb, masktb, start=False, stop=True),
                            reads=["identb", "masktb"], writes=[btok])
                    P.act(lambda e, pt=pt, bank=bank, c0=c0: e.activation(pt[:, c0:T], bank[:, c0:T], AF.Exp),
                          reads=[btok], writes=[pttok])

                    def pv(j=j, pt=pt, pttok=pttok, c0=c0):
                        P.pe(lambda e: e.matmul(OT[:, c0:T], vsb[:, j, :], pt[:, c0:T], start=(j == 0),
                                                stop=(j == nkb - 1)),
                             reads=[pttok, f"v{j}"], writes=[ottok])
                        P.pe(lambda e: e.matmul(Lb[:, c0:T], onesb, pt[:, c0:T], start=(j == 0),
                                                stop=(j == nkb - 1)),
                             reads=[pttok, "onesb"], writes=["ps5"])
                    pend.append(pv)
                    if len(pend) > 2:
                        pend.pop(0)()
                    if fillers:
                        fillers.pop(0)()
                for pv_ in pend:
                    pv_()
                while fillers:
                    fillers.pop(0)()
                P.act(lambda e: e.activation(rl, Lb[:], AF.Ln), reads=["ps5"], writes=["R1"])
                P.act(lambda e: e.activation(rl, rl, AF.Exp, scale=-1.0), reads=["R1"], writes=["R1"])
                ogb, ogtok = og[G % 2], f"og{G % 2}"
                P.dve(lambda e: e.tensor_tensor(ogb, OT[:], rl, ALU.mult),
                      reads=[ottok, "R1"], writes=[ogtok])
                P.dma(SP, lambda e, h=h: e.dma_start(out=oT_d[:, h, g0:g1], in_=ogb),
                      reads=[ogtok], writes=[f"oT_d{G}"], key=ogtok)

            def group_fillers(G):
                fl = []
                tails = []
                for i in range(4 * G, 4 * G + 4):
                    chunks, tail = pass1_chunks(i)
                    for ch in chunks:
                        fl.append(ch)
                        tails = [(tl, n - 1) for tl, n in tails]
                        while tails and tails[0][1] <= 0:
                            fl.append(tails.pop(0)[0])
                    tails.append((tail, 2))
                fl += [tl for tl, _ in tails]
                return fl

            for f_ in group_fillers(0):
                f_()
            for G in range(NTB):
                fillers = group_fillers(G + 1) if G + 1 < NTB else []
                pass2(G, fillers)
        P.barrier()
        A.reset(m)

    def mla_o_phase(l):
        jl = l - 2
        m = A.mark()
        wo = A.alloc([NH, D], BF16)
        ob = [A.alloc([NH, T], BF16) for _ in range(2)]
        P.dma(POOL, lambda e: e.dma_start(out=wo, in_=w_o[jl].rearrange("(h p) n -> p h n", p=128)),
              writes=["wo"], key="wo")
        ny = 0
        for tb in range(NTB):
            t0, t1 = tb * T, (tb + 1) * T
            o_ = ob[tb % 2]
            otok = f"ob{tb % 2}"
            P.dma(SP, lambda e, o_=o_, t0=t0, t1=t1: e.dma_start(out=o_, in_=oT_d[:, :, t0:t1]),
                  writes=[otok], key=otok)
            for c in range(NCH):
                yb = ps[ny % 2]
                ytok = f"ps{ny % 2}"
                ny += 1
                for h in range(NH):
                    P.pe(lambda e, yb=yb, o_=o_, h=h, c=c: e.matmul(
                        yb[:], wo[:, h, c * 128:(c + 1) * 128], o_[:, h, :], start=(h == 0), stop=(h == NH - 1)),
                        reads=["wo", otok], writes=[ytok])
                P.dve(lambda e, yb=yb, c=c, t0=t0, t1=t1: e.scalar_tensor_tensor(
                    xT[:, c, t0:t1], yb[:], mcol(l, 2, c), xT[:, c, t0:t1], ALU.mult, ALU.add),
                    reads=[ytok, "mods"] + xtok(c, t0, t1), writes=xtok(c, t0, t1))
        P.barrier()
        A.reset(m)

    phases = []
    for l in range(DEPTH):
        if l < 2:
            phases.append((f"mix{l}", lambda l=l: pool_phase(l)))
        else:
            phases.append((f"q{l}", lambda l=l: mla_q_phase(l)))
            phases.append((f"attn{l}", lambda l=l: mla_attn_phase(l)))
            phases.append((f"mix{l}", lambda l=l: mla_o_phase(l)))
        phases.append((f"ffn{l}", lambda l=l: ffn_phase(l)))
        if l == 1:
            phases.append(("kv", kv_phase))
    if stop_after == "init":
        dump_x(False)
        return finish()
    for tag, fn in phases:
        fn()
        if stop_after == tag:
            dump_x(False)
            return finish()
    dump_x(True)
    return finish()


_PROG = {}


def _in_maps(inputs):
    cst = _consts()
    mg = _maskg()
    maps = []
    f32 = lambda a: np.ascontiguousarray(np.asarray(a, np.float32))
    shared = {k: f32(inputs[k]) for k in ["mod_w", "pool_w", "w_dkv", "w_uk", "w_uv", "w_dq", "w_uq", "w_o",
                                          "w_up", "w_down"]}
    x = np.asarray(inputs["x"], np.float32)
    pos = np.asarray(inputs["positions"], np.int32)
    for b in range(8):
        mp = dict(shared)
        mp["x"] = np.ascontiguousarray(x[b])
        mp["pos"] = np.ascontiguousarray(pos[b:b + 1])
        mp["vecs"] = _build_vecs(inputs, b)
        mp["cst"] = cst
        mp["maskg"] = mg
        maps.append(mp)
    return maps


def kernel(**inputs):
    if "full" not in _PROG:
        _PROG["full"] = build_program(None)
    nc = _PROG["full"]
    res = run_bass_kernel_spmd(nc, _in_maps(inputs), core_ids=list(range(8)))
    return np.stack([np.asarray(r["y"], np.float32) for r in res.results], axis=0)
```
